# Optimizing a Trainium2 kernel written in Bass

```python
import jax, jax.numpy as jnp
from jax import lax
import numpy as np

D_MODEL = 1024
BATCH = 8
SEQ = 4096
DEPTH = 1

CHUNK = 64
EPS = 1e-6

POOL_WIDTH = D_MODEL
POOL_WINDOWS = (2, 4, 8, 16)
POOL_GROUPS = len(POOL_WINDOWS)
POOL_GROUP_DIM = POOL_WIDTH // POOL_GROUPS

GLA_HEADS = 4
GLA_KEY_DIM = D_MODEL // 2
GLA_VAL_DIM = D_MODEL
GLA_HEAD_K = GLA_KEY_DIM // GLA_HEADS
GLA_HEAD_V = GLA_VAL_DIM // GLA_HEADS
GLA_GATE_RANK = 16
GLA_GATE_NORMALIZER = 16.0

SPLITS = (
    POOL_WIDTH,
    POOL_WIDTH,
    GLA_KEY_DIM,
    GLA_KEY_DIM,
    GLA_VAL_DIM,
    GLA_VAL_DIM,
    GLA_GATE_RANK,
    D_MODEL,
    D_MODEL,
)
IN_WIDTH = sum(SPLITS)
SPLIT_POINTS = tuple(int(v) for v in np.cumsum(SPLITS)[:-1])

kernel_name = "hybrid_pool_gla_gated_merge"


def rmsnorm(x, gain):
    xf = x.astype(jnp.float32)
    y = xf * lax.rsqrt(jnp.mean(xf * xf, axis=-1, keepdims=True) + EPS)
    return (y * gain.astype(jnp.float32)).astype(x.dtype)


def causal_multiscale_pool(u):
    B, S, _ = u.shape
    ug = u.reshape(B, S, POOL_GROUPS, POOL_GROUP_DIM).astype(jnp.float32)
    cs = jnp.cumsum(ug, axis=1)
    t = jnp.arange(1, S + 1, dtype=jnp.float32)
    outs = []
    for gi, w in enumerate(POOL_WINDOWS):
        csg = cs[:, :, gi]
        prev = jnp.pad(csg, ((0, 0), (w, 0), (0, 0)))[:, :S]
        cnt = jnp.minimum(t, float(w))[None, :, None]
        outs.append((csg - prev) / cnt)
    mean = jnp.stack(outs, axis=2)
    return (mean - ug).astype(u.dtype)


def gla_chunked(q, k, v, log_a):
    B, S, H, dk = q.shape
    dv = v.shape[-1]
    N = S // CHUNK

    def to_chunks(t):
        return t.reshape(B, N, CHUNK, H, -1).transpose(0, 3, 1, 2, 4)

    qc = to_chunks(q.astype(jnp.float32)) * (dk ** -0.5)
    kc = to_chunks(k.astype(jnp.float32))
    vc = to_chunks(v.astype(jnp.float32))
    gc = to_chunks(log_a.astype(jnp.float32))
    b = jnp.cumsum(gc, axis=3)
    b_last = b[:, :, :, -1:, :]
    q_dec = qc * jnp.exp(b)
    k_inv = kc * jnp.exp(-b)
    k_to_end = kc * jnp.exp(b_last - b)
    decay_chunk = jnp.exp(b_last[:, :, :, 0, :])

    mask = jnp.tril(jnp.ones((CHUNK, CHUNK), dtype=bool))
    scores = jnp.einsum('bhnid,bhnjd->bhnij', q_dec, k_inv)
    scores = jnp.where(mask, scores, 0.0)
    o_intra = jnp.einsum('bhnij,bhnjv->bhniv', scores, vc)

    def step(state, inp):
        q_n, k_n, v_n, dec_n = inp
        o_n = jnp.einsum('bhid,bhdv->bhiv', q_n, state)
        state = state * dec_n[..., None] + jnp.einsum('bhjd,bhjv->bhdv', k_n, v_n)
        return state, o_n

    xs = (q_dec.transpose(2, 0, 1, 3, 4), k_to_end.transpose(2, 0, 1, 3, 4),
          vc.transpose(2, 0, 1, 3, 4), decay_chunk.transpose(2, 0, 1, 3))
    s0 = jnp.zeros((B, H, dk, dv), jnp.float32)
    _, o_inter = lax.scan(step, s0, xs)
    o = o_intra + o_inter.transpose(1, 2, 0, 3, 4)
    return o.transpose(0, 2, 3, 1, 4).reshape(B, S, H, dv).astype(q.dtype)


def hybrid_layer(x, c, g_norm, w_ada, b_ada, w_in, w_pool_group, pool_scale,
                 w_alpha_up, b_alpha, g_gla_head, w_pool_out, w_gla_out, w_out):
    B, S, D = x.shape
    mod = jax.nn.silu(c) @ w_ada + b_ada
    shift, scale, gate = jnp.split(mod, 3, axis=-1)
    h = rmsnorm(x, g_norm) * (1.0 + scale[:, None]) + shift[:, None]

    z = h @ w_in
    (pv, pg, q, k, v, gg, a_low, mg_pool, mg_gla) = jnp.split(z, SPLIT_POINTS, axis=-1)

    pooled = causal_multiscale_pool(pv)
    mixed = jnp.einsum('bsgc,gcd->bsgd', pooled, w_pool_group).reshape(B, S, POOL_WIDTH)
    y_pool = mixed * pool_scale * jax.nn.silu(pg)

    log_a = jax.nn.log_sigmoid((a_low @ w_alpha_up + b_alpha).astype(jnp.float32)) / GLA_GATE_NORMALIZER
    o = gla_chunked(q.reshape(B, S, GLA_HEADS, GLA_HEAD_K),
                    k.reshape(B, S, GLA_HEADS, GLA_HEAD_K),
                    v.reshape(B, S, GLA_HEADS, GLA_HEAD_V),
                    log_a.reshape(B, S, GLA_HEADS, GLA_HEAD_K))
    o = rmsnorm(o, g_gla_head).reshape(B, S, GLA_VAL_DIM)
    y_gla = o * jax.nn.silu(gg)

    merged = (jax.nn.sigmoid(mg_pool) * (y_pool @ w_pool_out)
              + jax.nn.sigmoid(mg_gla) * (y_gla @ w_gla_out))
    out = merged @ w_out
    return x + gate[:, None] * out


def setup_inputs(seed: int = 0) -> dict:
    key = jax.random.key(seed)
    ks = jax.random.split(key, 16)
    D = D_MODEL
    f32 = jnp.float32
    nrm = lambda k, shape, s: (jax.random.normal(k, shape, f32) * s)
    return {
        "x": nrm(ks[0], (BATCH, SEQ, D), 1.0),
        "c": nrm(ks[1], (BATCH, D), 1.0),
        "g_norm": 1.0 + nrm(ks[2], (DEPTH, D), 0.02),
        "w_ada": nrm(ks[3], (DEPTH, D, 3 * D), 0.5 * D ** -0.5),
        "b_ada": nrm(ks[4], (DEPTH, 3 * D), 0.02),
        "w_in": nrm(ks[5], (DEPTH, D, IN_WIDTH), D ** -0.5),
        "w_pool_group": nrm(ks[6], (DEPTH, POOL_GROUPS, POOL_GROUP_DIM, POOL_GROUP_DIM), POOL_GROUP_DIM ** -0.5),
        "pool_scale": 1.0 + nrm(ks[7], (DEPTH, POOL_WIDTH), 0.1),
        "w_alpha_up": nrm(ks[8], (DEPTH, GLA_GATE_RANK, GLA_KEY_DIM), GLA_GATE_RANK ** -0.5),
        "b_alpha": nrm(ks[9], (DEPTH, GLA_KEY_DIM), 0.1),
        "g_gla_head": 1.0 + nrm(ks[10], (DEPTH, GLA_HEAD_V), 0.02),
        "w_pool_out": nrm(ks[11], (DEPTH, POOL_WIDTH, D), POOL_WIDTH ** -0.5),
        "w_gla_out": nrm(ks[12], (DEPTH, GLA_VAL_DIM, D), GLA_VAL_DIM ** -0.5),
        "w_out": nrm(ks[13], (DEPTH, D, D), D ** -0.5),
        "g_final": 1.0 + nrm(ks[14], (D,), 0.02),
    }


def reference(x, c, g_norm, w_ada, b_ada, w_in, w_pool_group, pool_scale,
              w_alpha_up, b_alpha, g_gla_head, w_pool_out, w_gla_out, w_out, g_final):
    h = x
    for l in range(DEPTH):
        h = hybrid_layer(h, c, g_norm[l], w_ada[l], b_ada[l], w_in[l], w_pool_group[l],
                         pool_scale[l], w_alpha_up[l], b_alpha[l], g_gla_head[l],
                         w_pool_out[l], w_gla_out[l], w_out[l])
    return rmsnorm(h, g_final)
```

```python
import bisect
import math
from contextlib import ExitStack

import numpy as np
import concourse.bass as bass
import concourse.mybir as mybir
from concourse.bass_utils import run_bass_kernel_spmd

F32 = mybir.dt.float32
BF16 = mybir.dt.bfloat16
ALU = mybir.AluOpType
AF = mybir.ActivationFunctionType
AX = mybir.AxisListType

CELL = 64
_DTSIZE = {F32: 4, BF16: 2}


def _dtsize(dt):
    return _DTSIZE[dt]


def ap_cells(ap):
    name = ap.tensor.name
    es = _dtsize(ap.dtype)
    dims = list(ap.ap)
    pstep = dims[0][0]
    off = ap.offset % pstep if pstep > 0 else ap.offset
    free = dims[1:]
    if not free:
        free = [(1, 1)]
    starts = [off]
    for (st, cnt) in free[:-1]:
        if st != 0 and cnt > 1:
            starts = [s + st * i for s in starts for i in range(cnt)]
    lst, lcnt = free[-1]
    span = (lcnt - 1) * abs(lst) + 1 if lst != 0 else 1
    cells = set()
    for s in starts:
        b0 = s * es
        b1 = (s + span) * es
        cells.update(range(b0 // CELL, (b1 - 1) // CELL + 1))
    return name, cells


class Prog:
    CQ = ("pe", "act", "dve", "pool")

    def __init__(self, nc):
        self.nc = nc
        self.ops = []
        self.state = {}
        self.count = {q: 0 for q in self.CQ}
        self.known = {q: {} for q in ("pe", "act", "dve", "pool", "sp")}
        self.dma_val = {}
        self.signal = {q: set() for q in self.CQ}

    def _keys(self, regions):
        keys = []
        for r in regions:
            if isinstance(r, tuple):
                keys.append(r)
            else:
                name, cells = ap_cells(r)
                keys.extend((name, c) for c in cells)
        return keys

    def _add(self, q, fn, reads, writes, ev, is_dma, extra_waits=()):
        rk = self._keys(reads)
        wk = self._keys(writes)
        deps = {}

        def need(e):
            if e is None:
                return
            eq, ei = e
            if deps.get(eq, 0) < ei:
                deps[eq] = ei

        for k in rk:
            st = self.state.get(k)
            if st is not None:
                need(st[0])
        for k in wk:
            st = self.state.get(k)
            if st is not None:
                need(st[0])
                for eq, ei in st[1].items():
                    need((eq, ei))
        for e in extra_waits:
            need(e)
        waits = []
        kn = self.known[q]
        for eq, ei in deps.items():
            if eq == q:
                if q == "pe":
                    continue
            if kn.get(eq, 0) >= ei:
                continue
            kn[eq] = ei
            waits.append((eq, ei))
            if eq in self.signal:
                self.signal[eq].add(ei)
        for k in rk:
            st = self.state.get(k)
            if st is None:
                st = [None, {}]
                self.state[k] = st
            if st[1].get(ev[0], 0) < ev[1]:
                st[1][ev[0]] = ev[1]
        for k in wk:
            self.state[k] = [ev, {}]
        self.ops.append(dict(q=q, fn=fn, waits=waits, ev=ev, dma=is_dma))

    def op(self, q, fn, reads=(), writes=(), extra_waits=()):
        self.count[q] += 1
        ev = (q, self.count[q])
        self._add(q, fn, reads, writes, ev, False, extra_waits)
        return ev

    def dma(self, q, sem, fn, reads=(), writes=(), n=1):
        name = "dma:" + sem
        self.dma_val[name] = self.dma_val.get(name, 0) + 16 * n
        ev = (name, self.dma_val[name])
        self._add(q, fn, reads, writes, ev, True)
        return ev

    def wait(self, q, ev):
        self.ops.append(dict(q=q, fn=None, waits=[ev], ev=None, dma=False))
        if ev[0] in self.signal:
            self.signal[ev[0]].add(ev[1])

    def emit(self):
        nc = self.nc
        ranks = {q: sorted(s) for q, s in self.signal.items()}

        def val(e):
            eq, ei = e
            if eq in ranks:
                return bisect.bisect_right(ranks[eq], ei)
            return ei

        with ExitStack() as es:
            sems = {}
            for q in self.CQ:
                sems[q] = es.enter_context(nc.semaphore("s_" + q))
            for name in self.dma_val:
                sems[name] = es.enter_context(nc.semaphore("d_" + name[4:]))
            block = es.enter_context(nc.Block())

            def run(q, eng):
                for o in self.ops:
                    if o["q"] != q:
                        continue
                    for w in o["waits"]:
                        eng.wait_ge(sems[w[0]], val(w))
                    if o["fn"] is None:
                        continue
                    r = o["fn"](eng)
                    ev = o["ev"]
                    if o["dma"]:
                        for ins in r:
                            ins.then_inc(sems[ev[0]], 16)
                    elif ev[1] in self.signal[ev[0]]:
                        r.then_inc(sems[ev[0]], 1)

            @block.tensor
            def _(eng):
                run("pe", eng)

            @block.scalar
            def _(eng):
                run("act", eng)

            @block.vector
            def _(eng):
                run("dve", eng)

            @block.gpsimd
            def _(eng):
                run("pool", eng)

            @block.sync
            def _(eng):
                run("sp", eng)


D = 1024
T = 512
NSUB = T // 128
NSLOT = 5
EPS = 1e-6
W_IN_COLS = 7184
COL = dict(pv=0, pg=1024, q=2048, k=2560, v=3072, gg=4096, al=5120, mgp=5136, mgg=6160)
BLOCKS = [("v0", "w_in", 3072), ("v1", "w_in", 3584), ("gg0", "w_in", 4096), ("gg1", "w_in", 4608),
          ("q", "w_in", 2048), ("k", "w_in", 2560),
          ("pv0", "w_in", 0), ("pg0", "w_in", 1024), ("pv1", "w_in", 512), ("pg1", "w_in", 1536),
          ("mgp0", "w_in", 5136), ("wpo0", "w_pool_out", 0), ("mgp1", "w_in", 5648), ("wpo1", "w_pool_out", 512),
          ("mgg0", "w_in", 6160), ("wgo0", "w_gla_out", 0), ("mgg1", "w_in", 6672), ("wgo1", "w_gla_out", 512),
          ("wo0", "w_out", 0), ("wo1", "w_out", 512)]
NBLK = len(BLOCKS)


class Arena:
    def __init__(self, t):
        self.t = t
        self.off = 0

    def alloc(self, nbytes):
        off = (self.off + 63) // 64 * 64
        self.off = off + nbytes
        return off

    def view(self, off, shape, dtype):
        nfree = 1
        for s in shape[1:]:
            nfree *= s
        nbytes = nfree * _dtsize(dtype)
        assert off % 4 == 0 and nbytes % 4 == 0
        ap = self.t[0:shape[0], off // 4:(off + nbytes) // 4]
        if dtype != F32:
            ap = ap.bitcast(dtype)
        if len(shape) == 3:
            ap = ap.rearrange("p (a b) -> p a b", b=shape[2])
        elif len(shape) == 4:
            ap = ap.rearrange("p (a b c) -> p a b c", b=shape[2], c=shape[3])
        return ap

    def new(self, shape, dtype):
        nfree = 1
        for s in shape[1:]:
            nfree *= s
        return self.view(self.alloc(nfree * _dtsize(dtype)), shape, dtype)


def build_nc(NT, stop=None):
    nc = bass.Bass("TRN2", target_bir_lowering=False)
    NTOK = NT * T
    dr = {}

    def din(name, shape):
        dr[name] = nc.dram_tensor(name, list(shape), F32, kind="ExternalInput").ap()

    din("x", [NTOK, D])
    din("c", [D])
    din("g_norm", [D])
    din("w_ada", [D, 3 * D])
    din("b_ada", [3 * D])
    din("w_in", [D, W_IN_COLS])
    din("w_pool_group", [4, 256, 256])
    din("pool_scale", [D])
    din("w_alpha_up", [16, 512])
    din("b_alpha", [512])
    din("g_gla_head", [256])
    din("w_pool_out", [D, D])
    din("w_gla_out", [D, D])
    din("w_out", [D, D])
    din("g_final", [D])
    y_out = nc.dram_tensor("y", [NTOK, D], F32, kind="ExternalOutput").ap()
    scr = nc.dram_tensor("wscr", [NBLK, 128, 8 * 512], BF16, kind="Internal").ap()

    with ExitStack() as es:
        SB = es.enter_context(nc.sbuf_tensor("arena", [128, 53200], F32))
        PS = es.enter_context(nc.psum_tensor("ps", [128, 8 * 512], F32))
        A = Arena(SB)
        P = Prog(nc)

        class _Stop(Exception):
            pass

        def finish(dumps):
            ev = None
            for r0, ap in dumps:
                n = ap.shape[-1]
                qn = "sp" if ap.dtype == F32 else "pool"
                ev = P.dma(qn, "dbg", lambda e, r0=r0, ap=ap, n=n: [e.dma_start(out=y_out[r0:r0 + ap.shape[0], 0:n], in_=ap)], reads=[ap])
            if ev is not None:
                P.wait("sp", ev)
            P.emit()
            raise _Stop()

        pst = {"p": 0}

        def bank():
            b = pst["p"]
            pst["p"] = (b + 1) % 8
            return PS[:, b * 512:(b + 1) * 512]

        def bank2():
            if pst["p"] % 2:
                pst["p"] = (pst["p"] + 1) % 8
            b = pst["p"]
            pst["p"] = (b + 2) % 8
            return PS[:, b * 512:(b + 2) * 512]

        ident = A.new([128, 128], BF16)
        trimask = A.new([128, 128], BF16)
        cmask = A.new([128, 512], F32)
        mhalf = A.new([128, 16], F32)
        g1_bc = A.new([128, D], F32)
        gate_bc = A.new([128, D], F32)
        gfin_bc = A.new([128, D], F32)
        shift_col = A.new([128, 8], F32)
        nb_alpha = A.new([128, 4], F32)
        g_col = A.new([128, 2], F32)
        gh_bc = A.new([128, 256], F32)
        c_col = A.new([128, 8], F32)
        sc_bf = A.new([128, 8], BF16)
        w_up = A.new([16, 512], BF16)
        wal = A.new([128, 8, 16], BF16)
        wpg = A.new([128, 4, 2, 256], BF16)
        invc = A.new([128, 4, 16], F32)
        hist = A.new([128, 8, 16], F32)
        A_f = A.new([128, 4, 256], F32)
        S_bf = A.new([128, 4, 256], BF16)
        stat = A.new([128, 256], F32)
        decay = A.new([128, 4, NSUB], F32)
        one_c = A.new([128, 16], F32)
        xt = [A.new([128, D], F32) for _ in range(2)]
        junk = A.new([128, D], BF16)
        hn = [A.new([128, D], BF16) for _ in range(2)]
        hT = [A.new([128, 8, T], BF16) for _ in range(2)]
        t1 = A.new([128, 8, T], BF16)
        ktok = [A.new([128, 4, 128], BF16) for _ in range(2)]
        scs = [A.new([128, 4, 128], BF16) for _ in range(2)]
        on = [A.new([128, D], BF16) for _ in range(2)]
        offA = A.alloc(24 * 1024)
        o = offA
        vtok = A.view(o, [128, NSUB, D], BF16); o += 8192
        sgg = A.view(o, [128, 8, T], BF16); o += 8192
        gcum = A.view(o, [128, 4, T], F32)
        qd = A.view(o, [128, 4, T], BF16); o += 4096
        kinvT = A.view(o, [128, 4, T], BF16); o += 4096
        offB = A.alloc(25 * 1024)
        o = offB
        alT = A.view(o, [16, T], BF16); o += 1024
        gl = A.view(o, [128, 4, T], F32); o += 8192
        geb = A.view(o, [128, 4, T], F32); o += 8192
        genb = A.view(o, [128, 4, T], F32); o += 8192
        assert o - offB <= 25 * 1024
        o = offB
        pvh = []
        for _ in range(2):
            pvh.append(A.view(o, [128, 2, 528], F32)); o += 4224
        tA = A.view(o, [128, 2, 528], F32); o += 4224
        tB = A.view(o, [128, 2, 528], F32); o += 4224
        pooled = []
        for _ in range(2):
            pooled.append(A.view(o, [128, 2, T], BF16)); o += 2048
        spg = []
        for _ in range(2):
            spg.append(A.view(o, [128, 2, T], BF16)); o += 2048
        assert o - offB <= 25 * 1024
        o = offB
        sgg2 = []
        for _ in range(2):
            sgg2.append(A.view(o, [128, 4, T], BF16)); o += 4096
        tmpG = []
        for _ in range(2):
            tmpG.append(A.view(o, [128, T], F32)); o += 2048
        o += 4096
        merged = A.view(o, [128, 8, T], BF16); o += 8192
        assert o - offB <= 25 * 1024
        offC = A.alloc(16 * 1024)
        o = offC
        ypool = A.view(o, [128, 8, T], BF16); o += 8192
        sgp = []
        for _ in range(2):
            sgp.append(A.view(o, [128, 4, T], BF16)); o += 4096
        yglaT = A.new([128, 8, T], BF16)
        xt = xt + [A.view(offB + 14336, [128, D], F32), A.view(offB + 18432, [128, D], F32)]
        yo = [A.new([128, D], F32) for _ in range(4)]
        slots = [A.new([128, 8, 512], BF16) for _ in range(NSLOT)]
        assert A.off <= 53200 * 4, A.off

        blk_state = {"n": 0}
        src_view = {nm: dr[nm].rearrange("(k p) n -> p k n", p=128) for nm in ("w_in", "w_pool_out", "w_gla_out", "w_out")}
        total_blocks = NT * NBLK

        def issue_block(n):
            if n >= total_blocks:
                return
            ti, bi = divmod(n, NBLK)
            name, src, c0 = BLOCKS[bi]
            sl = slots[n % NSLOT]
            sem = ("slotS%d" if ti == 0 else "slotH%d") % (n % NSLOT)
            if ti == 0:
                P.dma("pool", sem, lambda e: [e.dma_start(out=sl, in_=src_view[src][:, :, c0:c0 + 512])], reads=list(first_issue_dep), writes=[sl])
                if NT > 1:
                    P.dma("sp", "wb%d" % (n % NSLOT), lambda e: [e.dma_start(out=scr[bi], in_=sl.rearrange("p k n -> p (k n)"))],
                          reads=[sl], writes=[("dr", "wscr", bi)])
            else:
                P.dma("sp", sem, lambda e: [e.dma_start(out=sl.rearrange("p k n -> p (k n)"), in_=scr[bi])],
                      reads=[("dr", "wscr", bi)], writes=[sl])

        cur = {"n": 0, "issued": 0}
        first_issue_dep = []

        def next_block(expect):
            n = cur["n"]
            assert BLOCKS[n % NBLK][0] == expect, (BLOCKS[n % NBLK][0], expect)
            cur["n"] = n + 1
            return slots[n % NSLOT]

        def release_blocks():
            while cur["issued"] < min(cur["n"] + NSLOT, total_blocks):
                issue_block(cur["issued"])
                cur["issued"] += 1

        xcnt = {"n": 0}
        xpend = {}

        def XA1(ti, s, xi=None):
            r0 = ti * T + s * 128
            if xi is None:
                xi = xcnt["n"] % 2
                xcnt["n"] += 1
            xb = xt[xi]
            P.dma("sp", "xt%d" % xi, lambda e, xb=xb, r0=r0: [e.dma_start(out=xb, in_=dr["x"][r0:r0 + 128, :])], writes=[xb])
            ss = stat[:, 16 * xi:16 * xi + 1]
            rs = stat[:, 16 * xi + 1:16 * xi + 2]
            P.op("act", lambda e, xb=xb, ss=ss: e.activation(out=junk, in_=xb, func=AF.Square, accum_out=ss), reads=[xb], writes=[junk, ss])
            P.op("pool", lambda e, ss=ss, rs=rs: e.tensor_scalar(out=rs, in0=ss, scalar1=1.0 / D, scalar2=EPS, op0=ALU.mult, op1=ALU.add), reads=[ss], writes=[rs])
            P.op("pool", lambda e, rs=rs: e.tensor_tensor(out=rs, in0=rs, in1=mhalf[:, 0:1], op=ALU.pow), reads=[rs, mhalf], writes=[rs])
            xpend[(ti, s, "a")] = xi

        def XA2(ti, s):
            xi = xpend.pop((ti, s, "a"))
            xb = xt[xi]
            hn_ = hn[xi % 2]
            rs = stat[:, 16 * xi + 1:16 * xi + 2]
            P.op("dve", lambda e, xb=xb, hn_=hn_, rs=rs: e.scalar_tensor_tensor(out=hn_, in0=xb, scalar=rs, in1=g1_bc, op0=ALU.mult, op1=ALU.mult),
                 reads=[xb, rs, g1_bc], writes=[hn_])
            xpend[(ti, s)] = hn_

        def XA(ti, s):
            XA1(ti, s)
            XA2(ti, s)

        def XB(ti, s):
            hn_ = xpend.pop((ti, s))
            hTt = hT[ti % 2]
            pb = bank().bitcast(BF16).rearrange("p (k t) -> p k t", t=128)
            for k in range(8):
                P.op("pe", lambda e, pb=pb, k=k, hn_=hn_: e.transpose(out=pb[:, k, :], in_=hn_[:, k * 128:(k + 1) * 128], identity=ident),
                     reads=[hn_[:, k * 128:(k + 1) * 128], ident], writes=[pb[:, k, :]])
            dst = hTt[:, :, s * 128:(s + 1) * 128]
            P.op("dve", lambda e, pb=pb, dst=dst: e.tensor_tensor(out=dst, in0=pb, in1=shift_col.unsqueeze(2).to_broadcast([128, 8, 128]), op=ALU.add),
                 reads=[pb, shift_col], writes=[dst])

        def x_stage(ti):
            for s in range(NSUB):
                XA(ti, s)
                XB(ti, s)

        def col_load(dst, src, sem):
            P.dma("sp", sem, lambda e: [e.dma_start(out=dst, in_=src, allow_slow_non_contiguous=True)], writes=[dst])

        col_load(c_col, dr["c"].rearrange("(k p) -> p k", p=128), "i0")
        yo23 = A.view(yo[2].offset * 4, [128, 8, 512], BF16)
        sgp01 = A.view(sgp[0].offset * 2, [128, 8, 512], BF16)
        ada_stage = [hT[0], hT[1], t1, yglaT, yo23, sgp01]
        for j in range(6):
            sl = ada_stage[j]
            P.dma("pool", "ada%d" % j,
                  lambda e, sl=sl, j=j: [e.dma_start(out=sl, in_=dr["w_ada"].rearrange("(k p) n -> p k n", p=128)[:, :, j * 512:(j + 1) * 512])],
                  writes=[sl])
        P.op("dve", lambda e: e.memset(mhalf, -0.5), writes=[mhalf])
        for s_ in range(NSUB):
            XA1(0, s_, xi=s_)
        col_load(nb_alpha, dr["b_alpha"].rearrange("(h p) -> p h", p=128), "i1")
        col_load(g_col, dr["g_gla_head"].rearrange("(h p) -> p h", p=128), "i2")
        P.dma("pool", "i3", lambda e: [e.dma_start(out=w_up, in_=dr["w_alpha_up"][:, :])], writes=[w_up])
        P.dma("pool", "i4", lambda e: [e.dma_start(out=wal, in_=dr["w_in"].rearrange("(k p) n -> p k n", p=128)[:, :, COL["al"]:COL["al"] + 16],
                                                   allow_slow_non_contiguous=True)], writes=[wal])
        rowbase = offA
        b_ada_row = A.view(rowbase, [1, 3 * D], F32)
        mod_row = A.view(rowbase + 12288, [1, 3 * D], F32)
        offR = offB
        gn_row = A.view(offR, [1, D], F32)
        gf_row = A.view(offR + 4096, [1, D], F32)
        ps_row = A.view(offR + 8192, [1, D], F32)
        P.dma("sp", "i5", lambda e: [e.dma_start(out=b_ada_row, in_=dr["b_ada"].rearrange("(o n) -> o n", o=1))], writes=[b_ada_row])
        P.dma("sp", "i6", lambda e: [e.dma_start(out=gn_row, in_=dr["g_norm"].rearrange("(o n) -> o n", o=1))], writes=[gn_row])
        P.dma("sp", "i7", lambda e: [e.dma_start(out=gf_row, in_=dr["g_final"].rearrange("(o n) -> o n", o=1))], writes=[gf_row])
        P.dma("sp", "i8", lambda e: [e.dma_start(out=ps_row, in_=dr["pool_scale"].rearrange("(o n) -> o n", o=1))], writes=[ps_row])
        gh_row = A.view(offR + 12288, [1, 256], F32)
        P.dma("sp", "i10", lambda e: [e.dma_start(out=gh_row, in_=dr["g_gla_head"].rearrange("(o n) -> o n", o=1))], writes=[gh_row])
        wpg_f = A.view(offC, [128, 4, 2, 256], F32)
        for g in range(4):
            P.dma("sp", "i9_%d" % g, lambda e, g=g: [e.dma_start(out=wpg_f[:, g, :, :], in_=dr["w_pool_group"][g].rearrange("(k p) d -> p k d", p=128))],
                  writes=[wpg_f[:, g, :, :]])

        tmpc = yo[0]
        tz = tmpc[:, 0:128]
        to = tmpc[:, 128:256]
        tif = tmpc[:, 256:384]
        ttf = tmpc[:, 384:512]
        P.op("pool", lambda e: e.memset(tz, 0.0), writes=[tz])
        P.op("pool", lambda e: e.memset(to, 1.0), writes=[to])
        P.op("pool", lambda e: e.affine_select(out=tif, in_=tz, pattern=[[-1, 128]], compare_op=ALU.not_equal, fill=1.0,
                                               base=0, channel_multiplier=1), reads=[tz], writes=[tif])
        P.op("pool", lambda e: e.tensor_copy(out=ident, in_=tif), reads=[tif], writes=[ident])
        P.op("pool", lambda e: e.affine_select(out=ttf, in_=to, pattern=[[1, 128]], compare_op=ALU.is_ge, fill=0.0,
                                               base=0, channel_multiplier=-1), reads=[to], writes=[ttf])
        P.op("pool", lambda e: e.tensor_copy(out=trimask, in_=ttf), reads=[ttf], writes=[trimask])
        P.op("dve", lambda e: e.memset(cmask, 1.0), writes=[cmask])
        cm3 = cmask.rearrange("p (c t) -> p c t", t=128)
        P.op("dve", lambda e: e.memset(cm3[:, :, 0:1], 0.0), writes=[cmask])
        P.op("dve", lambda e: e.memset(one_c, 1.0), writes=[one_c])
        P.op("dve", lambda e: e.memset(hist, 0.0), writes=[hist])
        P.op("dve", lambda e: e.memset(A_f, 0.0), writes=[A_f])
        P.op("dve", lambda e: e.memset(S_bf, 0.0), writes=[S_bf])
        for gi, w in enumerate((2, 4, 8, 16)):
            P.op("dve", lambda e, gi=gi, w=w: e.memset(invc[:, gi, :], 1.0 / w), writes=[invc[:, gi, :]])
            for t in range(w - 1):
                P.op("dve", lambda e, gi=gi, t=t: e.memset(invc[:, gi, t:t + 1], 1.0 / (t + 1)), writes=[invc[:, gi, t:t + 1]])
        first_issue_dep.append(ada_stage[5])
        release_blocks()
        first_issue_dep.clear()
        P.op("act", lambda e: e.activation(out=sc_bf, in_=c_col, func=AF.Silu), reads=[c_col], writes=[sc_bf])
        P.op("dve", lambda e: e.tensor_scalar(out=nb_alpha, in0=nb_alpha, scalar1=-1.0, scalar2=None, op0=ALU.mult), reads=[nb_alpha], writes=[nb_alpha])

        ones_row = one_c[0:1, 0:1]
        orow = yo[1][0:1, 0:128]
        P.op("dve", lambda e: e.memset(orow, 1.0), writes=[orow])

        def bcast_row(row512):
            pb = bank()
            P.op("pe", lambda e, pb=pb: e.matmul(pb, lhsT=orow, rhs=row512, start=True, stop=True), reads=[orow, row512], writes=[pb])
            return pb

        gn_bc = yo[0]
        for h in range(2):
            hs = slice(h * 512, (h + 1) * 512)
            pb = bcast_row(gn_row[:, hs])
            P.op("act", lambda e, pb=pb, hs=hs: e.copy(out=gn_bc[:, hs], in_=pb), reads=[pb], writes=[gn_bc[:, hs]])
            pb = bcast_row(gf_row[:, hs])
            P.op("act", lambda e, pb=pb, hs=hs: e.copy(out=gfin_bc[:, hs], in_=pb), reads=[pb], writes=[gfin_bc[:, hs]])
        pbg = bank()
        P.op("pe", lambda e, pbg=pbg: e.matmul(pbg[:, 0:256], lhsT=orow, rhs=gh_row, start=True, stop=True), reads=[orow, gh_row], writes=[pbg])
        P.op("act", lambda e, pbg=pbg: e.copy(out=gh_bc, in_=pbg[:, 0:256]), reads=[pbg], writes=[gh_bc])
        for h in range(2):
            pb = bcast_row(ps_row[:, h * 512:(h + 1) * 512])
            for gg_ in range(2):
                g = 2 * h + gg_
                for kc in range(2):
                    P.op("dve", lambda e, pb=pb, g=g, gg_=gg_, kc=kc: e.tensor_tensor(out=wpg[:, g, kc, :], in0=pb[:, gg_ * 256:(gg_ + 1) * 256],
                                                                                     in1=wpg_f[:, g, kc, :], op=ALU.mult),
                         reads=[pb, wpg_f[:, g, kc, :]], writes=[wpg[:, g, kc, :]])

        for j in range(6):
            sl = ada_stage[j]
            pb = bank()
            for k in range(8):
                P.op("pe", lambda e, pb=pb, sl=sl, k=k: e.matmul(pb[0:1, :], lhsT=sc_bf[:, k:k + 1], rhs=sl[:, k, :], start=(k == 0), stop=(k == 7)),
                     reads=[sc_bf, sl[:, k, :]], writes=[pb])
            P.op("dve", lambda e, pb=pb, j=j: e.tensor_tensor(out=mod_row[:, j * 512:(j + 1) * 512], in0=pb[0:1, :],
                                                              in1=b_ada_row[:, j * 512:(j + 1) * 512], op=ALU.add),
                 reads=[pb, b_ada_row[:, j * 512:(j + 1) * 512]], writes=[mod_row[:, j * 512:(j + 1) * 512]])
        for h in range(2):
            hs = slice(h * 512, (h + 1) * 512)
            pb = bcast_row(mod_row[:, D + h * 512:D + (h + 1) * 512])
            P.op("dve", lambda e, pb=pb, hs=hs: e.scalar_tensor_tensor(out=g1_bc[:, hs], in0=pb, scalar=1.0, in1=gn_bc[:, hs], op0=ALU.add, op1=ALU.mult),
                 reads=[pb, gn_bc[:, hs]], writes=[g1_bc[:, hs]])
        pb = bank()
        for k in range(8):
            P.op("pe", lambda e, pb=pb, k=k: e.matmul(pb[:, k:k + 1], lhsT=mod_row[:, k * 128:(k + 1) * 128], rhs=ones_row, start=True, stop=True),
                 reads=[mod_row[:, k * 128:(k + 1) * 128], ones_row], writes=[pb[:, k:k + 1]])
        P.op("act", lambda e, pb=pb: e.copy(out=shift_col, in_=pb[:, 0:8]), reads=[pb], writes=[shift_col])
        for h in range(2):
            hs = slice(h * 512, (h + 1) * 512)
            pb = bcast_row(mod_row[:, 2 * D + h * 512:2 * D + (h + 1) * 512])
            P.op("act", lambda e, pb=pb, hs=hs: e.copy(out=gate_bc[:, hs], in_=pb), reads=[pb], writes=[gate_bc[:, hs]])
        LNQ = math.log(128.0 ** -0.5)

        def mm_group(pb, slot, j, rhs3):
            for k in range(8):
                P.op("pe", lambda e, k=k: e.matmul(pb, lhsT=slot[:, k, j * 128:(j + 1) * 128], rhs=rhs3[:, k, :], start=(k == 0), stop=(k == 7)),
                     reads=[slot[:, k, j * 128:(j + 1) * 128], rhs3[:, k, :]], writes=[pb])

        def tile_prog(ti):
            hTt = hT[ti % 2]

            pb = bank()
            for k in range(8):
                P.op("pe", lambda e, pb=pb, k=k: e.matmul(pb[0:16, :], lhsT=wal[:, k, :], rhs=hTt[:, k, :], start=(k == 0), stop=(k == 7)),
                     reads=[wal[:, k, :], hTt[:, k, :]], writes=[pb])
            P.op("act", lambda e, pb=pb: e.copy(out=alT, in_=pb[0:16, :]), reads=[pb], writes=[alT])
            for hd in range(4):
                pb = bank()
                P.op("pe", lambda e, pb=pb, hd=hd: e.matmul(pb, lhsT=w_up[:, hd * 128:(hd + 1) * 128], rhs=alT, start=True, stop=True),
                     reads=[w_up, alT], writes=[pb])
                P.op("act", lambda e, pb=pb, hd=hd: e.activation(out=gl[:, hd, :], in_=pb, func=AF.Exp, scale=-1.0, bias=nb_alpha[:, hd:hd + 1]),
                     reads=[pb, nb_alpha], writes=[gl[:, hd, :]])
            P.op("act", lambda e: e.activation(out=gl, in_=gl, func=AF.Ln, scale=1.0, bias=1.0), reads=[gl], writes=[gl])
            for hd in range(4):
                P.op("dve", lambda e, hd=hd: e.tensor_tensor_scan(out=gcum[:, hd, :], data0=cmask, data1=gl[:, hd, :], initial=0.0,
                                                                  op0=ALU.mult, op1=ALU.add),
                     reads=[cmask, gl[:, hd, :]], writes=[gcum[:, hd, :]])
            P.op("act", lambda e: e.activation(out=geb, in_=gcum, func=AF.Exp, scale=-1.0 / 16, bias=LNQ), reads=[gcum], writes=[geb])
            P.op("act", lambda e: e.activation(out=genb, in_=gcum, func=AF.Exp, scale=1.0 / 16), reads=[gcum], writes=[genb])
            P.op("act", lambda e: e.activation(out=decay, in_=gcum[:, :, 127::128], func=AF.Exp, scale=-1.0 / 16), reads=[gcum], writes=[decay])
            if stop == "gates":
                finish([(0, geb[:, 0, :]), (128, genb[:, 3, :]), (256, gcum[:, 1, :]), (384, decay.rearrange("p h s -> p (h s)"))])

            s_v = [next_block("v0"), next_block("v1")]
            for s in range(NSUB):
                pp = bank2()
                for cb in range(2):
                    for k in range(8):
                        P.op("pe", lambda e, pp=pp, cb=cb, k=k, s=s: e.matmul(pp[:, cb * 512:(cb + 1) * 512], lhsT=hTt[:, k, s * 128:(s + 1) * 128],
                                                                             rhs=s_v[cb][:, k, :], start=(k == 0), stop=(k == 7)),
                             reads=[hTt[:, k, s * 128:(s + 1) * 128], s_v[cb][:, k, :]], writes=[pp[:, cb * 512:(cb + 1) * 512]])
                P.op("act", lambda e, pp=pp, s=s: e.copy(out=vtok[:, s, :], in_=pp), reads=[pp], writes=[vtok[:, s, :]])
                if deferred:
                    deferred.pop(0)()
            release_blocks()
            for blk in range(2):
                s_g = next_block("gg%d" % blk)
                for jj in range(2):
                    pp = bank2()
                    mm_group(pp[:, 0:512], s_g, 2 * jj, hTt)
                    mm_group(pp[:, 512:1024], s_g, 2 * jj + 1, hTt)
                    dst = sgg[:, 4 * blk + 2 * jj:4 * blk + 2 * jj + 2, :]
                    P.op("act", lambda e, pp=pp, dst=dst: e.activation(out=dst, in_=pp.rearrange("p (a t) -> p a t", t=512), func=AF.Silu), reads=[pp], writes=[dst])
                release_blocks()
            s_q = next_block("q")
            for hd in range(4):
                pb = bank()
                mm_group(pb, s_q, hd, hTt)
                P.op("dve", lambda e, pb=pb, hd=hd: e.tensor_tensor(out=qd[:, hd, :], in0=pb, in1=geb[:, hd, :], op=ALU.mult),
                     reads=[pb, geb[:, hd, :]], writes=[qd[:, hd, :]])
            s_k = next_block("k")
            for hd in range(4):
                pb = bank()
                mm_group(pb, s_k, hd, hTt)
                P.op("dve", lambda e, pb=pb, hd=hd: e.tensor_tensor(out=kinvT[:, hd, :], in0=pb, in1=genb[:, hd, :], op=ALU.mult),
                     reads=[pb, genb[:, hd, :]], writes=[kinvT[:, hd, :]])
            release_blocks()
            if stop == "glain":
                finish([(0, qd[:, 1, :]), (128, kinvT[:, 2, :]), (256, vtok[:, 1, 0:512]), (384, sgg[:, 6, :])])

            def filler():
                pend = []
                pend_comb = []

                def pool_group_mm(g):
                    pg_ = pooled[g % 2]
                    sp_ = spg[g % 2]
                    for dc in range(2):
                        pb = bank()
                        for kc in range(2):
                            P.op("pe", lambda e, pb=pb, g=g, kc=kc, dc=dc: e.matmul(pb, lhsT=wpg[:, g, kc, dc * 128:(dc + 1) * 128], rhs=pg_[:, kc, :],
                                                                                   start=(kc == 0), stop=(kc == 1)),
                                 reads=[wpg[:, g, kc, dc * 128:(dc + 1) * 128], pg_[:, kc, :]], writes=[pb])
                        P.op("dve", lambda e, pb=pb, g=g, dc=dc: e.tensor_tensor(out=ypool[:, 2 * g + dc, :], in0=pb, in1=sp_[:, dc, :], op=ALU.mult),
                             reads=[pb, sp_[:, dc, :]], writes=[ypool[:, 2 * g + dc, :]])

                for blk in range(2):
                    s_pv = next_block("pv%d" % blk)
                    s_pg = next_block("pg%d" % blk)
                    for gg_ in range(2):
                        g = 2 * blk + gg_
                        w = 2 ** (g + 1)
                        pv_ = pvh[g % 2]
                        while pend_comb:
                            pend_comb.pop(0)()
                        P.op("pool", lambda e, pv_=pv_, g=g: e.tensor_copy(out=pv_[:, :, 0:16], in_=hist[:, 2 * g:2 * g + 2, :]),
                             reads=[hist[:, 2 * g:2 * g + 2, :]], writes=[pv_[:, :, 0:16]])
                        for cc in range(2):
                            j = 2 * gg_ + cc
                            pb = bank()
                            mm_group(pb, s_pv, j, hTt)
                            P.op("act", lambda e, pb=pb, pv_=pv_, cc=cc: e.copy(out=pv_[:, cc, 16:528], in_=pb), reads=[pb], writes=[pv_[:, cc, 16:528]])
                            yield
                        P.op("pool", lambda e, pv_=pv_, g=g: e.tensor_copy(out=hist[:, 2 * g:2 * g + 2, :], in_=pv_[:, :, 512:528]),
                             reads=[pv_[:, :, 512:528]], writes=[hist[:, 2 * g:2 * g + 2, :]])
                        src = pv_
                        lo = 16 - (w - 1)
                        m = 1
                        bufs = [tA, tB]
                        bi = 0
                        while m < w:
                            lo2 = lo + m
                            dst = bufs[bi]
                            eng = "dve"
                            P.op(eng, lambda e, dst=dst, src=src, lo2=lo2, m=m: e.tensor_tensor(out=dst[:, :, lo2:528], in0=src[:, :, lo2:528],
                                                                                               in1=src[:, :, lo2 - m:528 - m], op=ALU.add),
                                 reads=[src[:, :, lo2 - m:528]], writes=[dst[:, :, lo2:528]])
                            src = dst
                            lo = lo2
                            m *= 2
                            bi ^= 1
                        pg_ = pooled[g % 2]

                        def combine(src=src, pv_=pv_, pg_=pg_, w=w, g=g):
                            P.op("dve", lambda e: e.scalar_tensor_tensor(out=pg_, in0=src[:, :, 16:528], scalar=1.0 / w, in1=pv_[:, :, 16:528],
                                                                         op0=ALU.mult, op1=ALU.subtract),
                                 reads=[src[:, :, 16:528], pv_[:, :, 16:528]], writes=[pg_])
                            if ti == 0:
                                nfix = w - 1
                                tmpf = stat[:, 160:192].rearrange("p (c t) -> p c t", t=16)
                                P.op("dve", lambda e: e.tensor_tensor(out=tmpf[:, :, 0:nfix], in0=src[:, :, 16:16 + nfix],
                                                                      in1=invc[:, g:g + 1, 0:nfix].to_broadcast([128, 2, nfix]), op=ALU.mult),
                                     reads=[src[:, :, 16:16 + nfix], invc], writes=[tmpf])
                                P.op("dve", lambda e: e.tensor_tensor(out=pg_[:, :, 0:nfix], in0=tmpf[:, :, 0:nfix], in1=pv_[:, :, 16:16 + nfix], op=ALU.subtract),
                                     reads=[tmpf, pv_[:, :, 16:16 + nfix]], writes=[pg_[:, :, 0:nfix]])

                        pend_comb.append(combine)
                        sp_ = spg[g % 2]
                        for cc in range(2):
                            j = 2 * gg_ + cc
                            pb = bank()
                            mm_group(pb, s_pg, j, hTt)
                            P.op("act", lambda e, pb=pb, sp_=sp_, cc=cc: e.activation(out=sp_[:, cc, :], in_=pb, func=AF.Silu), reads=[pb], writes=[sp_[:, cc, :]])
                            yield
                        if pend:
                            pool_group_mm(pend.pop())
                            yield
                        pend.append(g)
                    release_blocks()
                while pend_comb:
                    pend_comb.pop(0)()
                if stop == "pool":
                    if pend:
                        pool_group_mm(pend.pop())
                    finish([(0, ypool[:, 0, :]), (128, ypool[:, 3, :]), (256, ypool[:, 7, :])])
                for blk in range(2):
                    s_m = next_block("mgp%d" % blk)
                    for j in range(4):
                        pb = bank()
                        mm_group(pb, s_m, j, hTt)
                        P.op("act", lambda e, pb=pb, blk=blk, j=j: e.activation(out=sgp[blk][:, j, :], in_=pb, func=AF.Sigmoid), reads=[pb], writes=[sgp[blk][:, j, :]])
                        yield
                    if pend:
                        pool_group_mm(pend.pop())
                        yield
                    s_w = next_block("wpo%d" % blk)
                    for j in range(4):
                        pb = bank()
                        mm_group(pb, s_w, j, ypool)
                        P.op("dve", lambda e, pb=pb, blk=blk, j=j: e.tensor_tensor(out=t1[:, 4 * blk + j, :], in0=pb, in1=sgp[blk][:, j, :], op=ALU.mult),
                             reads=[pb, sgp[blk][:, j, :]], writes=[t1[:, 4 * blk + j, :]])
                        yield
                    release_blocks()
                if stop == "pphase":
                    finish([(0, t1[:, 0, :]), (128, t1[:, 5, :])])

            fill = filler()

            def adv(n):
                for _ in range(n):
                    try:
                        next(fill)
                    except StopIteration:
                        return

            def S1(s):
                cs = slice(s * 128, (s + 1) * 128)
                kt = ktok[s % 2]
                sc_ = scs[s % 2]
                pT = bank().bitcast(BF16)[:, 0:512].rearrange("p (h t) -> p h t", t=128)
                for hd in range(4):
                    P.op("pe", lambda e, pT=pT, hd=hd, cs=cs: e.transpose(out=pT[:, hd, :], in_=kinvT[:, hd, cs], identity=ident),
                         reads=[kinvT[:, hd, cs], ident], writes=[pT[:, hd, :]])
                P.op("act", lambda e, pT=pT, kt=kt: e.copy(out=kt, in_=pT), reads=[pT], writes=[kt])
                pS = bank().rearrange("p (h t) -> p h t", t=128)
                for hd in range(4):
                    P.op("pe", lambda e, pS=pS, hd=hd, cs=cs: e.matmul(pS[:, hd, :], lhsT=kinvT[:, hd, cs], rhs=qd[:, hd, cs], start=True, stop=True),
                         reads=[kinvT[:, hd, cs], qd[:, hd, cs]], writes=[pS[:, hd, :]])
                P.op("dve", lambda e, pS=pS, sc_=sc_: e.tensor_tensor(out=sc_, in0=pS, in1=trimask.unsqueeze(1).to_broadcast([128, 4, 128]), op=ALU.mult),
                     reads=[pS, trimask], writes=[sc_])

            def S2(s):
                cs = slice(s * 128, (s + 1) * 128)
                kt = ktok[s % 2]
                sc_ = scs[s % 2]
                on_ = on[s % 2]
                pO = bank2()
                for hd in range(4):
                    oo = pO[:, hd * 256:(hd + 1) * 256]
                    P.op("pe", lambda e, oo=oo, hd=hd, s=s, sc_=sc_: e.matmul(oo, lhsT=sc_[:, hd, :], rhs=vtok[:, s, hd * 256:(hd + 1) * 256], start=True, stop=False),
                         reads=[sc_[:, hd, :], vtok[:, s, hd * 256:(hd + 1) * 256]], writes=[oo])
                    P.op("pe", lambda e, oo=oo, hd=hd, cs=cs: e.matmul(oo, lhsT=qd[:, hd, cs], rhs=S_bf[:, hd, :], start=False, stop=True),
                         reads=[qd[:, hd, cs], S_bf[:, hd, :]], writes=[oo])
                pD = bank2()
                for hd in range(4):
                    dd = pD[:, hd * 256:(hd + 1) * 256]
                    P.op("pe", lambda e, dd=dd, hd=hd, s=s, kt=kt: e.matmul(dd, lhsT=kt[:, hd, :], rhs=vtok[:, s, hd * 256:(hd + 1) * 256], start=True, stop=True),
                         reads=[kt[:, hd, :], vtok[:, s, hd * 256:(hd + 1) * 256]], writes=[dd])
                osb = yo[s % 4]
                P.op("act", lambda e, pO=pO, osb=osb: e.copy(out=osb, in_=pO), reads=[pO], writes=[osb])
                A2 = A_f.rearrange("p h v -> p (h v)")
                P.op("dve", lambda e, pD=pD, A2=A2: e.tensor_tensor(out=A2, in0=pD, in1=A2, op=ALU.add), reads=[pD, A_f], writes=[A_f])
                for hd in range(4):
                    P.op("act", lambda e, hd=hd, s=s: e.activation(out=S_bf[:, hd, :], in_=A_f[:, hd, :], func=AF.Copy, scale=decay[:, hd, s:s + 1]),
                         reads=[A_f[:, hd, :], decay], writes=[S_bf[:, hd, :]])
                P.op("dve", lambda e, s=s: e.tensor_tensor(out=A_f, in0=A_f, in1=decay[:, :, s:s + 1].to_broadcast([128, 4, 256]), op=ALU.mult),
                     reads=[A_f, decay], writes=[A_f])
                ssq = stat[:, 64 + 16 * (s % 2):68 + 16 * (s % 2)]
                rsq = stat[:, 68 + 16 * (s % 2):72 + 16 * (s % 2)]
                for hd in range(4):
                    P.op("act", lambda e, osb=osb, hd=hd: e.activation(out=junk[:, hd * 256:(hd + 1) * 256], in_=osb[:, hd * 256:(hd + 1) * 256], func=AF.Square,
                                                                      accum_out=ssq[:, hd:hd + 1]),
                         reads=[osb[:, hd * 256:(hd + 1) * 256]], writes=[junk[:, hd * 256:(hd + 1) * 256], ssq[:, hd:hd + 1]])
                P.op("pool", lambda e: e.tensor_scalar(out=rsq, in0=ssq, scalar1=1.0 / 256, scalar2=EPS, op0=ALU.mult, op1=ALU.add), reads=[ssq], writes=[rsq])
                P.op("pool", lambda e: e.tensor_tensor(out=rsq, in0=rsq, in1=mhalf[:, 0:4], op=ALU.pow), reads=[rsq, mhalf], writes=[rsq])

            def S2b(s):
                on_ = on[s % 2]
                osb = yo[s % 4]
                rsq = stat[:, 68 + 16 * (s % 2):72 + 16 * (s % 2)]
                for hd in range(4):
                    hs = slice(hd * 256, (hd + 1) * 256)
                    P.op("dve", lambda e, osb=osb, on_=on_, hd=hd, hs=hs: e.scalar_tensor_tensor(out=on_[:, hs], in0=osb[:, hs], scalar=rsq[:, hd:hd + 1], in1=gh_bc,
                                                                                                op0=ALU.mult, op1=ALU.mult),
                         reads=[osb[:, hs], rsq, gh_bc], writes=[on_[:, hs]])

            def S3(s):
                cs = slice(s * 128, (s + 1) * 128)
                on_ = on[s % 2]
                pT2 = bank().bitcast(BF16).rearrange("p (k t) -> p k t", t=128)
                for kc in range(8):
                    P.op("pe", lambda e, pT2=pT2, kc=kc, on_=on_: e.transpose(out=pT2[:, kc, :], in_=on_[:, kc * 128:(kc + 1) * 128], identity=ident),
                         reads=[on_[:, kc * 128:(kc + 1) * 128], ident], writes=[pT2[:, kc, :]])
                P.op("dve", lambda e, pT2=pT2, cs=cs: e.tensor_tensor(out=yglaT[:, :, cs], in0=pT2, in1=sgg[:, :, cs], op=ALU.mult),
                     reads=[pT2, sgg[:, :, cs]], writes=[yglaT[:, :, cs]])

            nxt = ti + 1 < NT
            S1(0)
            if nxt:
                XA(ti + 1, 0)
            adv(2)
            for s in range(NSUB):
                S2(s)
                if s + 1 < NSUB:
                    S1(s + 1)
                adv(2)
                S2b(s)
                adv(2)
                if s >= 1:
                    S3(s - 1)
                if nxt:
                    XB(ti + 1, s)
                    if s + 1 < NSUB:
                        XA(ti + 1, s + 1)
                adv(4)
            adv(2)
            S3(NSUB - 1)
            adv(10 ** 6)
            if stop == "gla":
                finish([(0, yglaT[:, 0, :]), (128, yglaT[:, 3, :]), (256, yglaT[:, 6, :]), (384, A_f[:, 2, :])])
            for blk in range(2):
                s_m = next_block("mgg%d" % blk)
                for jj in range(2):
                    pp = bank2()
                    mm_group(pp[:, 0:512], s_m, 2 * jj, hTt)
                    mm_group(pp[:, 512:1024], s_m, 2 * jj + 1, hTt)
                    dst = sgg2[blk][:, 2 * jj:2 * jj + 2, :]
                    P.op("act", lambda e, pp=pp, dst=dst: e.activation(out=dst, in_=pp.rearrange("p (a t) -> p a t", t=512), func=AF.Sigmoid), reads=[pp], writes=[dst])
                s_w = next_block("wgo%d" % blk)
                for j in range(4):
                    pb = bank()
                    mm_group(pb, s_w, j, yglaT)
                    tg = tmpG[j % 2]
                    P.op("dve", lambda e, pb=pb, blk=blk, j=j, tg=tg: e.tensor_tensor(out=tg, in0=pb, in1=sgg2[blk][:, j, :], op=ALU.mult),
                         reads=[pb, sgg2[blk][:, j, :]], writes=[tg])
                    P.op("dve", lambda e, blk=blk, j=j, tg=tg: e.tensor_tensor(out=merged[:, 4 * blk + j, :], in0=tg, in1=t1[:, 4 * blk + j, :], op=ALU.add),
                         reads=[tg, t1[:, 4 * blk + j, :]], writes=[merged[:, 4 * blk + j, :]])
                release_blocks()
            s_o = [next_block("wo0"), next_block("wo1")]
            if ti == NT - 1:
                hfree = hT[(ti + 1) % 2]
                xlast = [xt[0], xt[1], A.view(hfree.offset * 2, [128, D], F32), A.view(hfree.offset * 2 + 4096, [128, D], F32)]
                for s in range(NSUB):
                    r0 = ti * T + s * 128
                    P.dma("sp", "xl%d" % s, lambda e, xl_=xlast[s], r0=r0: [e.dma_start(out=xl_, in_=dr["x"][r0:r0 + 128, :])], writes=[xlast[s]])
            for s in range(NSUB):
                r0 = ti * T + s * 128
                yo_ = yo[s % 4]
                pp = bank2()
                for cb in range(2):
                    for k in range(8):
                        P.op("pe", lambda e, pp=pp, cb=cb, k=k, s=s: e.matmul(pp[:, cb * 512:(cb + 1) * 512], lhsT=merged[:, k, s * 128:(s + 1) * 128],
                                                                             rhs=s_o[cb][:, k, :], start=(k == 0), stop=(k == 7)),
                             reads=[merged[:, k, s * 128:(s + 1) * 128], s_o[cb][:, k, :]], writes=[pp[:, cb * 512:(cb + 1) * 512]])
                P.op("dve", lambda e, pp=pp, yo_=yo_: e.tensor_tensor(out=yo_, in0=pp, in1=gate_bc, op=ALU.mult), reads=[pp, gate_bc], writes=[yo_])
                if ti == NT - 1:
                    P.op("dve", lambda e, yo_=yo_, xl_=xlast[s]: e.tensor_tensor(out=yo_, in0=yo_, in1=xl_, op=ALU.add), reads=[yo_, xlast[s]], writes=[yo_])
                else:
                    P.dma("pool", "ya%d" % (s % 4), lambda e, yo_=yo_, r0=r0: [e.dma_start(out=yo_, in_=dr["x"][r0:r0 + 128, :], accum_op=ALU.add)],
                          reads=[yo_], writes=[yo_])

            def final_sub(s, ti=ti, stage=None):
                r0 = ti * T + s * 128
                yo_ = yo[s % 4]
                ssy = stat[:, 96 + 16 * s:97 + 16 * s]
                rsy = stat[:, 97 + 16 * s:98 + 16 * s]
                if stage in (None, 0):
                    P.op("act", lambda e, yo_=yo_, ssy=ssy: e.activation(out=junk, in_=yo_, func=AF.Square, accum_out=ssy), reads=[yo_], writes=[junk, ssy])
                    P.op("pool", lambda e, ssy=ssy, rsy=rsy: e.tensor_scalar(out=rsy, in0=ssy, scalar1=1.0 / D, scalar2=EPS, op0=ALU.mult, op1=ALU.add), reads=[ssy], writes=[rsy])
                    P.op("pool", lambda e, rsy=rsy: e.tensor_tensor(out=rsy, in0=rsy, in1=mhalf[:, 0:1], op=ALU.pow), reads=[rsy, mhalf], writes=[rsy])
                if stage in (None, 1):
                    P.op("dve", lambda e, yo_=yo_, rsy=rsy: e.scalar_tensor_tensor(out=yo_, in0=yo_, scalar=rsy, in1=gfin_bc, op0=ALU.mult, op1=ALU.mult),
                         reads=[yo_, rsy, gfin_bc], writes=[yo_])
                if stage in (None, 2):
                    final_evs[s % 4] = P.dma("pool", "st%d" % (s % 4), lambda e, yo_=yo_, r0=r0: [e.dma_start(out=y_out[r0:r0 + 128, :], in_=yo_)], reads=[yo_])

            last_final[0] = final_sub
            for s in range(NSUB):
                deferred.append(lambda s=s: final_sub(s))
            release_blocks()

        final_evs = {}
        deferred = []
        last_final = [None]
        try:
            if stop == "setup":
                finish([(0, g1_bc), (128, gate_bc), (256, gfin_bc), (384, shift_col)])
            XA2(0, 0)
            for s_ in range(NSUB):
                if s_ + 1 < NSUB:
                    XA2(0, s_ + 1)
                XB(0, s_)
            if stop == "x0":
                finish([])
            for ti in range(NT):
                tile_prog(ti)
            deferred.clear()
            for st_ in range(3):
                for s_ in range(NSUB):
                    last_final[0](s_, stage=st_)
            for ev in final_evs.values():
                P.wait("pool", ev)
            P.emit()
        except _Stop:
            pass
    return nc


_NC_CACHE = {}


def _get_nc(NT):
    if NT not in _NC_CACHE:
        _NC_CACHE[NT] = build_nc(NT)
    return _NC_CACHE[NT]


def make_in_maps(inputs, NT, cores):
    f = lambda a: np.ascontiguousarray(np.asarray(a, dtype=np.float32))
    shared = dict(
        g_norm=f(inputs["g_norm"][0]), w_ada=f(inputs["w_ada"][0]), b_ada=f(inputs["b_ada"][0]), w_in=f(inputs["w_in"][0]),
        w_pool_group=f(inputs["w_pool_group"][0]), pool_scale=f(inputs["pool_scale"][0]), w_alpha_up=f(inputs["w_alpha_up"][0]),
        b_alpha=f(inputs["b_alpha"][0]), g_gla_head=f(inputs["g_gla_head"][0]), w_pool_out=f(inputs["w_pool_out"][0]),
        w_gla_out=f(inputs["w_gla_out"][0]), w_out=f(inputs["w_out"][0]), g_final=f(inputs["g_final"]))
    maps = []
    for b in cores:
        m = dict(shared)
        m["x"] = f(inputs["x"][b, :NT * T])
        m["c"] = f(inputs["c"][b])
        maps.append(m)
    return maps


def kernel(**inputs):
    NT = 4096 // T
    nc = _get_nc(NT)
    in_maps = make_in_maps(inputs, NT, list(range(8)))
    res = run_bass_kernel_spmd(nc, in_maps, core_ids=list(range(8)))
    out = np.stack([np.asarray(res.results[b]["y"], dtype=np.float32) for b in range(8)], axis=0)
    return out
```

```python
import bisect
import math
from contextlib import ExitStack

import numpy as np
import concourse.bass as bass
import concourse.mybir as mybir
from concourse.bass_utils import run_bass_kernel_spmd

F32 = mybir.dt.float32
BF16 = mybir.dt.bfloat16
ALU = mybir.AluOpType
AF = mybir.ActivationFunctionType
AX = mybir.AxisListType

CELL = 64
_DTSIZE = {F32: 4, BF16: 2}


def _dtsize(dt):
    return _DTSIZE[dt]


def ap_cells(ap):
    name = ap.tensor.name
    es = _dtsize(ap.dtype)
    dims = list(ap.ap)
    pstep = dims[0][0]
    off = ap.offset % pstep if pstep > 0 else ap.offset
    free = dims[1:]
    if not free:
        free = [(1, 1)]
    starts = [off]
    for (st, cnt) in free[:-1]:
        if st != 0 and cnt > 1:
            starts = [s + st * i for s in starts for i in range(cnt)]
    lst, lcnt = free[-1]
    span = (lcnt - 1) * abs(lst) + 1 if lst != 0 else 1
    cells = set()
    for s in starts:
        b0 = s * es
        b1 = (s + span) * es
        cells.update(range(b0 // CELL, (b1 - 1) // CELL + 1))
    return name, cells


class Prog:
    CQ = ("pe", "act", "dve", "pool")

    def __init__(self, nc):
        self.nc = nc
        self.ops = []
        self.state = {}
        self.count = {q: 0 for q in self.CQ}
        self.known = {q: {} for q in ("pe", "act", "dve", "pool", "sp")}
        self.dma_val = {}
        self.signal = {q: set() for q in self.CQ}

    def _keys(self, regions):
        keys = []
        for r in regions:
            if isinstance(r, tuple):
                keys.append(r)
            else:
                name, cells = ap_cells(r)
                keys.extend((name, c) for c in cells)
        return keys

    def _add(self, q, fn, reads, writes, ev, is_dma, extra_waits=()):
        rk = self._keys(reads)
        wk = self._keys(writes)
        deps = {}

        def need(e):
            if e is None:
                return
            eq, ei = e
            if deps.get(eq, 0) < ei:
                deps[eq] = ei

        for k in rk:
            st = self.state.get(k)
            if st is not None:
                need(st[0])
        for k in wk:
            st = self.state.get(k)
            if st is not None:
                need(st[0])
                for eq, ei in st[1].items():
                    need((eq, ei))
        for e in extra_waits:
            need(e)
        waits = []
        kn = self.known[q]
        for eq, ei in deps.items():
            if eq == q:
                if q == "pe":
                    continue
            if kn.get(eq, 0) >= ei:
                continue
            kn[eq] = ei
            waits.append((eq, ei))
            if eq in self.signal:
                self.signal[eq].add(ei)
        for k in rk:
            st = self.state.get(k)
            if st is None:
                st = [None, {}]
                self.state[k] = st
            if st[1].get(ev[0], 0) < ev[1]:
                st[1][ev[0]] = ev[1]
        for k in wk:
            self.state[k] = [ev, {}]
        self.ops.append(dict(q=q, fn=fn, waits=waits, ev=ev, dma=is_dma))

    def op(self, q, fn, reads=(), writes=(), extra_waits=()):
        self.count[q] += 1
        ev = (q, self.count[q])
        self._add(q, fn, reads, writes, ev, False, extra_waits)
        return ev

    def dma(self, q, sem, fn, reads=(), writes=(), n=1):
        name = "dma:" + sem
        self.dma_val[name] = self.dma_val.get(name, 0) + 16 * n
        ev = (name, self.dma_val[name])
        self._add(q, fn, reads, writes, ev, True)
        return ev

    def wait(self, q, ev):
        self.ops.append(dict(q=q, fn=None, waits=[ev], ev=None, dma=False))
        if ev[0] in self.signal:
            self.signal[ev[0]].add(ev[1])

    def emit(self):
        nc = self.nc
        ranks = {q: sorted(s) for q, s in self.signal.items()}

        def val(e):
            eq, ei = e
            if eq in ranks:
                return bisect.bisect_right(ranks[eq], ei)
            return ei

        with ExitStack() as es:
            sems = {}
            for q in self.CQ:
                sems[q] = es.enter_context(nc.semaphore("s_" + q))
            for name in self.dma_val:
                sems[name] = es.enter_context(nc.semaphore("d_" + name[4:]))
            block = es.enter_context(nc.Block())

            def run(q, eng):
                for o in self.ops:
                    if o["q"] != q:
                        continue
                    for w in o["waits"]:
                        eng.wait_ge(sems[w[0]], val(w))
                    if o["fn"] is None:
                        continue
                    r = o["fn"](eng)
                    ev = o["ev"]
                    if o["dma"]:
                        for ins in r:
                            ins.then_inc(sems[ev[0]], 16)
                    elif ev[1] in self.signal[ev[0]]:
                        r.then_inc(sems[ev[0]], 1)

            @block.tensor
            def _(eng):
                run("pe", eng)

            @block.scalar
            def _(eng):
                run("act", eng)

            @block.vector
            def _(eng):
                run("dve", eng)

            @block.gpsimd
            def _(eng):
                run("pool", eng)

            @block.sync
            def _(eng):
                run("sp", eng)


D = 1024
T = 512
NSUB = T // 128
NSLOT = 5
EPS = 1e-6
W_IN_COLS = 7184
COL = dict(pv=0, pg=1024, q=2048, k=2560, v=3072, gg=4096, al=5120, mgp=5136, mgg=6160)
BLOCKS = [("v0", "w_in", 3072), ("v1", "w_in", 3584), ("gg0", "w_in", 4096), ("gg1", "w_in", 4608),
          ("q", "w_in", 2048), ("k", "w_in", 2560),
          ("pv0", "w_in", 0), ("pg0", "w_in", 1024), ("pv1", "w_in", 512), ("pg1", "w_in", 1536),
          ("mgp0", "w_in", 5136), ("wpo0", "w_pool_out", 0), ("mgp1", "w_in", 5648), ("wpo1", "w_pool_out", 512),
          ("mgg0", "w_in", 6160), ("wgo0", "w_gla_out", 0), ("mgg1", "w_in", 6672), ("wgo1", "w_gla_out", 512),
          ("wo0", "w_out", 0), ("wo1", "w_out", 512)]
NBLK = len(BLOCKS)


class Arena:
    def __init__(self, t):
        self.t = t
        self.off = 0

    def alloc(self, nbytes):
        off = (self.off + 63) // 64 * 64
        self.off = off + nbytes
        return off

    def view(self, off, shape, dtype):
        nfree = 1
        for s in shape[1:]:
            nfree *= s
        nbytes = nfree * _dtsize(dtype)
        assert off % 4 == 0 and nbytes % 4 == 0
        ap = self.t[0:shape[0], off // 4:(off + nbytes) // 4]
        if dtype != F32:
            ap = ap.bitcast(dtype)
        if len(shape) == 3:
            ap = ap.rearrange("p (a b) -> p a b", b=shape[2])
        elif len(shape) == 4:
            ap = ap.rearrange("p (a b c) -> p a b c", b=shape[2], c=shape[3])
        return ap

    def new(self, shape, dtype):
        nfree = 1
        for s in shape[1:]:
            nfree *= s
        return self.view(self.alloc(nfree * _dtsize(dtype)), shape, dtype)


def build_nc(NT, stop=None):
    nc = bass.Bass("TRN2", target_bir_lowering=False)
    NTOK = NT * T
    dr = {}

    def din(name, shape):
        dr[name] = nc.dram_tensor(name, list(shape), F32, kind="ExternalInput").ap()

    din("x", [NTOK, D])
    din("c", [D])
    din("g_norm", [D])
    din("w_ada", [D, 3 * D])
    din("b_ada", [3 * D])
    din("w_in", [D, W_IN_COLS])
    din("w_pool_group", [4, 256, 256])
    din("pool_scale", [D])
    din("w_alpha_up", [16, 512])
    din("b_alpha", [512])
    din("g_gla_head", [256])
    din("w_pool_out", [D, D])
    din("w_gla_out", [D, D])
    din("w_out", [D, D])
    din("g_final", [D])
    y_out = nc.dram_tensor("y", [NTOK, D], F32, kind="ExternalOutput").ap()
    scr = nc.dram_tensor("wscr", [NBLK, 128, 8 * 512], BF16, kind="Internal").ap()

    with ExitStack() as es:
        SB = es.enter_context(nc.sbuf_tensor("arena", [128, 53200], F32))
        PS = es.enter_context(nc.psum_tensor("ps", [128, 8 * 512], F32))
        A = Arena(SB)
        P = Prog(nc)

        class _Stop(Exception):
            pass

        def finish(dumps):
            ev = None
            for r0, ap in dumps:
                n = ap.shape[-1]
                qn = "sp" if ap.dtype == F32 else "pool"
                ev = P.dma(qn, "dbg", lambda e, r0=r0, ap=ap, n=n: [e.dma_start(out=y_out[r0:r0 + ap.shape[0], 0:n], in_=ap)], reads=[ap])
            if ev is not None:
                P.wait("sp", ev)
            P.emit()
            raise _Stop()

        pst = {"p": 0}

        def bank():
            b = pst["p"]
            pst["p"] = (b + 1) % 8
            return PS[:, b * 512:(b + 1) * 512]

        def bank2():
            if pst["p"] % 2:
                pst["p"] = (pst["p"] + 1) % 8
            b = pst["p"]
            pst["p"] = (b + 2) % 8
            return PS[:, b * 512:(b + 2) * 512]

        ident = A.new([128, 128], BF16)
        trimask = A.new([128, 128], BF16)
        cmask = A.new([128, 512], F32)
        mhalf = A.new([128, 16], F32)
        g1_bc = A.new([128, D], F32)
        gate_bc = A.new([128, D], F32)
        gfin_bc = A.new([128, D], F32)
        shift_col = A.new([128, 8], F32)
        nb_alpha = A.new([128, 4], F32)
        g_col = A.new([128, 2], F32)
        gh_bc = A.new([128, 256], F32)
        c_col = A.new([128, 8], F32)
        sc_bf = A.new([128, 8], BF16)
        w_up = A.new([16, 512], BF16)
        wal = A.new([128, 8, 16], BF16)
        wpg = A.new([128, 4, 2, 256], BF16)
        invc = A.new([128, 4, 16], F32)
        hist = A.new([128, 8, 16], F32)
        A_f = A.new([128, 4, 256], F32)
        S_bf = A.new([128, 4, 256], BF16)
        stat = A.new([128, 256], F32)
        decay = A.new([128, 4, NSUB], F32)
        one_c = A.new([128, 16], F32)
        xt = [A.new([128, D], F32) for _ in range(2)]
        junk = A.new([128, D], BF16)
        hn = [A.new([128, D], BF16) for _ in range(2)]
        hT = [A.new([128, 8, T], BF16) for _ in range(2)]
        t1 = A.new([128, 8, T], BF16)
        ktok = [A.new([128, 4, 128], BF16) for _ in range(2)]
        scs = [A.new([128, 4, 128], BF16) for _ in range(2)]
        on = [A.new([128, D], BF16) for _ in range(2)]
        offA = A.alloc(24 * 1024)
        o = offA
        vtok = A.view(o, [128, NSUB, D], BF16); o += 8192
        sgg = A.view(o, [128, 8, T], BF16); o += 8192
        gcum = A.view(o, [128, 4, T], F32)
        qd = A.view(o, [128, 4, T], BF16); o += 4096
        kinvT = A.view(o, [128, 4, T], BF16); o += 4096
        offB = A.alloc(25 * 1024)
        o = offB
        alT = A.view(o, [16, T], BF16); o += 1024
        gl = A.view(o, [128, 4, T], F32); o += 8192
        geb = A.view(o, [128, 4, T], F32); o += 8192
        genb = A.view(o, [128, 4, T], F32); o += 8192
        assert o - offB <= 25 * 1024
        o = offB
        pvh = []
        for _ in range(2):
            pvh.append(A.view(o, [128, 2, 528], F32)); o += 4224
        tA = A.view(o, [128, 2, 528], F32); o += 4224
        tB = A.view(o, [128, 2, 528], F32); o += 4224
        pooled = []
        for _ in range(2):
            pooled.append(A.view(o, [128, 2, T], BF16)); o += 2048
        spg = []
        for _ in range(2):
            spg.append(A.view(o, [128, 2, T], BF16)); o += 2048
        assert o - offB <= 25 * 1024
        o = offB
        sgg2 = []
        for _ in range(2):
            sgg2.append(A.view(o, [128, 4, T], BF16)); o += 4096
        tmpG = []
        for _ in range(2):
            tmpG.append(A.view(o, [128, T], F32)); o += 2048
        o += 4096
        merged = A.view(o, [128, 8, T], BF16); o += 8192
        assert o - offB <= 25 * 1024
        offC = A.alloc(16 * 1024)
        o = offC
        ypool = A.view(o, [128, 8, T], BF16); o += 8192
        sgp = []
        for _ in range(2):
            sgp.append(A.view(o, [128, 4, T], BF16)); o += 4096
        yglaT = A.new([128, 8, T], BF16)
        xt = xt + [A.view(offB + 14336, [128, D], F32), A.view(offB + 18432, [128, D], F32)]
        yo = [A.new([128, D], F32) for _ in range(4)]
        slots = [A.new([128, 8, 512], BF16) for _ in range(NSLOT)]
        assert A.off <= 53200 * 4, A.off

        blk_state = {"n": 0}
        src_view = {nm: dr[nm].rearrange("(k p) n -> p k n", p=128) for nm in ("w_in", "w_pool_out", "w_gla_out", "w_out")}
        total_blocks = NT * NBLK

        def issue_block(n):
            if n >= total_blocks:
                return
            ti, bi = divmod(n, NBLK)
            name, src, c0 = BLOCKS[bi]
            sl = slots[n % NSLOT]
            sem = ("slotS%d" if ti == 0 else "slotH%d") % (n % NSLOT)
            if ti == 0:
                P.dma("pool", sem, lambda e: [e.dma_start(out=sl, in_=src_view[src][:, :, c0:c0 + 512])], reads=list(first_issue_dep), writes=[sl])
                if NT > 1:
                    P.dma("sp", "wb%d" % (n % NSLOT), lambda e: [e.dma_start(out=scr[bi], in_=sl.rearrange("p k n -> p (k n)"))],
                          reads=[sl], writes=[("dr", "wscr", bi)])
            else:
                P.dma("sp", sem, lambda e: [e.dma_start(out=sl.rearrange("p k n -> p (k n)"), in_=scr[bi])],
                      reads=[("dr", "wscr", bi)], writes=[sl])

        cur = {"n": 0, "issued": 0}
        first_issue_dep = []

        def next_block(expect):
            n = cur["n"]
            assert BLOCKS[n % NBLK][0] == expect, (BLOCKS[n % NBLK][0], expect)
            cur["n"] = n + 1
            return slots[n % NSLOT]

        def release_blocks():
            while cur["issued"] < min(cur["n"] + NSLOT, total_blocks):
                issue_block(cur["issued"])
                cur["issued"] += 1

        xcnt = {"n": 0}
        xpend = {}

        def XA1(ti, s, xi=None):
            r0 = ti * T + s * 128
            if xi is None:
                xi = xcnt["n"] % 2
                xcnt["n"] += 1
            xb = xt[xi]
            P.dma("sp", "xt%d" % xi, lambda e, xb=xb, r0=r0: [e.dma_start(out=xb, in_=dr["x"][r0:r0 + 128, :])], writes=[xb])
            ss = stat[:, 16 * xi:16 * xi + 1]
            rs = stat[:, 16 * xi + 1:16 * xi + 2]
            P.op("act", lambda e, xb=xb, ss=ss: e.activation(out=junk, in_=xb, func=AF.Square, accum_out=ss), reads=[xb], writes=[junk, ss])
            P.op("pool", lambda e, ss=ss, rs=rs: e.tensor_scalar(out=rs, in0=ss, scalar1=1.0 / D, scalar2=EPS, op0=ALU.mult, op1=ALU.add), reads=[ss], writes=[rs])
            P.op("pool", lambda e, rs=rs: e.tensor_tensor(out=rs, in0=rs, in1=mhalf[:, 0:1], op=ALU.pow), reads=[rs, mhalf], writes=[rs])
            xpend[(ti, s, "a")] = xi

        def XA2(ti, s):
            xi = xpend.pop((ti, s, "a"))
            xb = xt[xi]
            hn_ = hn[xi % 2]
            rs = stat[:, 16 * xi + 1:16 * xi + 2]
            P.op("dve", lambda e, xb=xb, hn_=hn_, rs=rs: e.scalar_tensor_tensor(out=hn_, in0=xb, scalar=rs, in1=g1_bc, op0=ALU.mult, op1=ALU.mult),
                 reads=[xb, rs, g1_bc], writes=[hn_])
            xpend[(ti, s)] = hn_

        def XA(ti, s):
            XA1(ti, s)
            XA2(ti, s)

        def XB(ti, s):
            hn_ = xpend.pop((ti, s))
            hTt = hT[ti % 2]
            pb = bank().bitcast(BF16).rearrange("p (k t) -> p k t", t=128)
            for k in range(8):
                P.op("pe", lambda e, pb=pb, k=k, hn_=hn_: e.transpose(out=pb[:, k, :], in_=hn_[:, k * 128:(k + 1) * 128], identity=ident),
                     reads=[hn_[:, k * 128:(k + 1) * 128], ident], writes=[pb[:, k, :]])
            dst = hTt[:, :, s * 128:(s + 1) * 128]
            P.op("dve", lambda e, pb=pb, dst=dst: e.tensor_tensor(out=dst, in0=pb, in1=shift_col.unsqueeze(2).to_broadcast([128, 8, 128]), op=ALU.add),
                 reads=[pb, shift_col], writes=[dst])

        def x_stage(ti):
            for s in range(NSUB):
                XA(ti, s)
                XB(ti, s)

        def col_load(dst, src, sem):
            P.dma("sp", sem, lambda e: [e.dma_start(out=dst, in_=src, allow_slow_non_contiguous=True)], writes=[dst])

        col_load(c_col, dr["c"].rearrange("(k p) -> p k", p=128), "i0")
        yo23 = A.view(yo[2].offset * 4, [128, 8, 512], BF16)
        sgp01 = A.view(sgp[0].offset * 2, [128, 8, 512], BF16)
        ada_stage = [hT[0], hT[1], t1, yglaT, yo23, sgp01]
        for j in range(6):
            sl = ada_stage[j]
            P.dma("pool", "ada%d" % j,
                  lambda e, sl=sl, j=j: [e.dma_start(out=sl, in_=dr["w_ada"].rearrange("(k p) n -> p k n", p=128)[:, :, j * 512:(j + 1) * 512])],
                  writes=[sl])
        P.op("dve", lambda e: e.memset(mhalf, -0.5), writes=[mhalf])
        for s_ in range(NSUB):
            XA1(0, s_, xi=s_)
        col_load(nb_alpha, dr["b_alpha"].rearrange("(h p) -> p h", p=128), "i1")
        col_load(g_col, dr["g_gla_head"].rearrange("(h p) -> p h", p=128), "i2")
        P.dma("pool", "i3", lambda e: [e.dma_start(out=w_up, in_=dr["w_alpha_up"][:, :])], writes=[w_up])
        P.dma("pool", "i4", lambda e: [e.dma_start(out=wal, in_=dr["w_in"].rearrange("(k p) n -> p k n", p=128)[:, :, COL["al"]:COL["al"] + 16],
                                                   allow_slow_non_contiguous=True)], writes=[wal])
        rowbase = offA
        b_ada_row = A.view(rowbase, [1, 3 * D], F32)
        mod_row = A.view(rowbase + 12288, [1, 3 * D], F32)
        offR = offB
        gn_row = A.view(offR, [1, D], F32)
        gf_row = A.view(offR + 4096, [1, D], F32)
        ps_row = A.view(offR + 8192, [1, D], F32)
        P.dma("sp", "i5", lambda e: [e.dma_start(out=b_ada_row, in_=dr["b_ada"].rearrange("(o n) -> o n", o=1))], writes=[b_ada_row])
        P.dma("sp", "i6", lambda e: [e.dma_start(out=gn_row, in_=dr["g_norm"].rearrange("(o n) -> o n", o=1))], writes=[gn_row])
        P.dma("sp", "i7", lambda e: [e.dma_start(out=gf_row, in_=dr["g_final"].rearrange("(o n) -> o n", o=1))], writes=[gf_row])
        P.dma("sp", "i8", lambda e: [e.dma_start(out=ps_row, in_=dr["pool_scale"].rearrange("(o n) -> o n", o=1))], writes=[ps_row])
        gh_row = A.view(offR + 12288, [1, 256], F32)
        P.dma("sp", "i10", lambda e: [e.dma_start(out=gh_row, in_=dr["g_gla_head"].rearrange("(o n) -> o n", o=1))], writes=[gh_row])
        wpg_f = A.view(offC, [128, 4, 2, 256], F32)
        for g in range(4):
            P.dma("sp", "i9_%d" % g, lambda e, g=g: [e.dma_start(out=wpg_f[:, g, :, :], in_=dr["w_pool_group"][g].rearrange("(k p) d -> p k d", p=128))],
                  writes=[wpg_f[:, g, :, :]])

        tmpc = yo[0]
        tz = tmpc[:, 0:128]
        to = tmpc[:, 128:256]
        tif = tmpc[:, 256:384]
        ttf = tmpc[:, 384:512]
        P.op("pool", lambda e: e.memset(tz, 0.0), writes=[tz])
        P.op("pool", lambda e: e.memset(to, 1.0), writes=[to])
        P.op("pool", lambda e: e.affine_select(out=tif, in_=tz, pattern=[[-1, 128]], compare_op=ALU.not_equal, fill=1.0,
                                               base=0, channel_multiplier=1), reads=[tz], writes=[tif])
        P.op("pool", lambda e: e.tensor_copy(out=ident, in_=tif), reads=[tif], writes=[ident])
        P.op("pool", lambda e: e.affine_select(out=ttf, in_=to, pattern=[[1, 128]], compare_op=ALU.is_ge, fill=0.0,
                                               base=0, channel_multiplier=-1), reads=[to], writes=[ttf])
        P.op("pool", lambda e: e.tensor_copy(out=trimask, in_=ttf), reads=[ttf], writes=[trimask])
        P.op("dve", lambda e: e.memset(cmask, 1.0), writes=[cmask])
        cm3 = cmask.rearrange("p (c t) -> p c t", t=128)
        P.op("dve", lambda e: e.memset(cm3[:, :, 0:1], 0.0), writes=[cmask])
        P.op("dve", lambda e: e.memset(one_c, 1.0), writes=[one_c])
        P.op("dve", lambda e: e.memset(hist, 0.0), writes=[hist])
        P.op("dve", lambda e: e.memset(A_f, 0.0), writes=[A_f])
        P.op("dve", lambda e: e.memset(S_bf, 0.0), writes=[S_bf])
        for gi, w in enumerate((2, 4, 8, 16)):
            P.op("dve", lambda e, gi=gi, w=w: e.memset(invc[:, gi, :], 1.0 / w), writes=[invc[:, gi, :]])
            for t in range(w - 1):
                P.op("dve", lambda e, gi=gi, t=t: e.memset(invc[:, gi, t:t + 1], 1.0 / (t + 1)), writes=[invc[:, gi, t:t + 1]])
        first_issue_dep.append(ada_stage[5])
        release_blocks()
        first_issue_dep.clear()
        P.op("act", lambda e: e.activation(out=sc_bf, in_=c_col, func=AF.Silu), reads=[c_col], writes=[sc_bf])
        P.op("dve", lambda e: e.tensor_scalar(out=nb_alpha, in0=nb_alpha, scalar1=-1.0, scalar2=None, op0=ALU.mult), reads=[nb_alpha], writes=[nb_alpha])

        ones_row = one_c[0:1, 0:1]
        orow = yo[1][0:1, 0:128]
        P.op("dve", lambda e: e.memset(orow, 1.0), writes=[orow])

        def bcast_row(row512):
            pb = bank()
            P.op("pe", lambda e, pb=pb: e.matmul(pb, lhsT=orow, rhs=row512, start=True, stop=True), reads=[orow, row512], writes=[pb])
            return pb

        gn_bc = yo[0]
        for h in range(2):
            hs = slice(h * 512, (h + 1) * 512)
            pb = bcast_row(gn_row[:, hs])
            P.op("act", lambda e, pb=pb, hs=hs: e.copy(out=gn_bc[:, hs], in_=pb), reads=[pb], writes=[gn_bc[:, hs]])
            pb = bcast_row(gf_row[:, hs])
            P.op("act", lambda e, pb=pb, hs=hs: e.copy(out=gfin_bc[:, hs], in_=pb), reads=[pb], writes=[gfin_bc[:, hs]])
        pbg = bank()
        P.op("pe", lambda e, pbg=pbg: e.matmul(pbg[:, 0:256], lhsT=orow, rhs=gh_row, start=True, stop=True), reads=[orow, gh_row], writes=[pbg])
        P.op("act", lambda e, pbg=pbg: e.copy(out=gh_bc, in_=pbg[:, 0:256]), reads=[pbg], writes=[gh_bc])
        for h in range(2):
            pb = bcast_row(ps_row[:, h * 512:(h + 1) * 512])
            for gg_ in range(2):
                g = 2 * h + gg_
                for kc in range(2):
                    P.op("dve", lambda e, pb=pb, g=g, gg_=gg_, kc=kc: e.tensor_tensor(out=wpg[:, g, kc, :], in0=pb[:, gg_ * 256:(gg_ + 1) * 256],
                                                                                     in1=wpg_f[:, g, kc, :], op=ALU.mult),
                         reads=[pb, wpg_f[:, g, kc, :]], writes=[wpg[:, g, kc, :]])

        for j in range(6):
            sl = ada_stage[j]
            pb = bank()
            for k in range(8):
                P.op("pe", lambda e, pb=pb, sl=sl, k=k: e.matmul(pb[0:1, :], lhsT=sc_bf[:, k:k + 1], rhs=sl[:, k, :], start=(k == 0), stop=(k == 7)),
                     reads=[sc_bf, sl[:, k, :]], writes=[pb])
            P.op("dve", lambda e, pb=pb, j=j: e.tensor_tensor(out=mod_row[:, j * 512:(j + 1) * 512], in0=pb[0:1, :],
                                                              in1=b_ada_row[:, j * 512:(j + 1) * 512], op=ALU.add),
                 reads=[pb, b_ada_row[:, j * 512:(j + 1) * 512]], writes=[mod_row[:, j * 512:(j + 1) * 512]])
        for h in range(2):
            hs = slice(h * 512, (h + 1) * 512)
            pb = bcast_row(mod_row[:, D + h * 512:D + (h + 1) * 512])
            P.op("dve", lambda e, pb=pb, hs=hs: e.scalar_tensor_tensor(out=g1_bc[:, hs], in0=pb, scalar=1.0, in1=gn_bc[:, hs], op0=ALU.add, op1=ALU.mult),
                 reads=[pb, gn_bc[:, hs]], writes=[g1_bc[:, hs]])
        pb = bank()
        for k in range(8):
            P.op("pe", lambda e, pb=pb, k=k: e.matmul(pb[:, k:k + 1], lhsT=mod_row[:, k * 128:(k + 1) * 128], rhs=ones_row, start=True, stop=True),
                 reads=[mod_row[:, k * 128:(k + 1) * 128], ones_row], writes=[pb[:, k:k + 1]])
        P.op("act", lambda e, pb=pb: e.copy(out=shift_col, in_=pb[:, 0:8]), reads=[pb], writes=[shift_col])
        for h in range(2):
            hs = slice(h * 512, (h + 1) * 512)
            pb = bcast_row(mod_row[:, 2 * D + h * 512:2 * D + (h + 1) * 512])
            P.op("act", lambda e, pb=pb, hs=hs: e.copy(out=gate_bc[:, hs], in_=pb), reads=[pb], writes=[gate_bc[:, hs]])
        LNQ = math.log(128.0 ** -0.5)

        def mm_group(pb, slot, j, rhs3):
            for k in range(8):
                P.op("pe", lambda e, k=k: e.matmul(pb, lhsT=slot[:, k, j * 128:(j + 1) * 128], rhs=rhs3[:, k, :], start=(k == 0), stop=(k == 7)),
                     reads=[slot[:, k, j * 128:(j + 1) * 128], rhs3[:, k, :]], writes=[pb])

        def tile_prog(ti):
            hTt = hT[ti % 2]

            pb = bank()
            for k in range(8):
                P.op("pe", lambda e, pb=pb, k=k: e.matmul(pb[0:16, :], lhsT=wal[:, k, :], rhs=hTt[:, k, :], start=(k == 0), stop=(k == 7)),
                     reads=[wal[:, k, :], hTt[:, k, :]], writes=[pb])
            P.op("act", lambda e, pb=pb: e.copy(out=alT, in_=pb[0:16, :]), reads=[pb], writes=[alT])
            for hd in range(4):
                pb = bank()
                P.op("pe", lambda e, pb=pb, hd=hd: e.matmul(pb, lhsT=w_up[:, hd * 128:(hd + 1) * 128], rhs=alT, start=True, stop=True),
                     reads=[w_up, alT], writes=[pb])
                P.op("act", lambda e, pb=pb, hd=hd: e.activation(out=gl[:, hd, :], in_=pb, func=AF.Exp, scale=-1.0, bias=nb_alpha[:, hd:hd + 1]),
                     reads=[pb, nb_alpha], writes=[gl[:, hd, :]])
            P.op("act", lambda e: e.activation(out=gl, in_=gl, func=AF.Ln, scale=1.0, bias=1.0), reads=[gl], writes=[gl])
            for hd in range(4):
                P.op("dve", lambda e, hd=hd: e.tensor_tensor_scan(out=gcum[:, hd, :], data0=cmask, data1=gl[:, hd, :], initial=0.0,
                                                                  op0=ALU.mult, op1=ALU.add),
                     reads=[cmask, gl[:, hd, :]], writes=[gcum[:, hd, :]])
            P.op("act", lambda e: e.activation(out=geb, in_=gcum, func=AF.Exp, scale=-1.0 / 16, bias=LNQ), reads=[gcum], writes=[geb])
            P.op("act", lambda e: e.activation(out=genb, in_=gcum, func=AF.Exp, scale=1.0 / 16), reads=[gcum], writes=[genb])
            P.op("act", lambda e: e.activation(out=decay, in_=gcum[:, :, 127::128], func=AF.Exp, scale=-1.0 / 16), reads=[gcum], writes=[decay])
            if stop == "gates":
                finish([(0, geb[:, 0, :]), (128, genb[:, 3, :]), (256, gcum[:, 1, :]), (384, decay.rearrange("p h s -> p (h s)"))])

            s_v = [next_block("v0"), next_block("v1")]
            for s in range(NSUB):
                pp = bank2()
                for cb in range(2):
                    for k in range(8):
                        P.op("pe", lambda e, pp=pp, cb=cb, k=k, s=s: e.matmul(pp[:, cb * 512:(cb + 1) * 512], lhsT=hTt[:, k, s * 128:(s + 1) * 128],
                                                                             rhs=s_v[cb][:, k, :], start=(k == 0), stop=(k == 7)),
                             reads=[hTt[:, k, s * 128:(s + 1) * 128], s_v[cb][:, k, :]], writes=[pp[:, cb * 512:(cb + 1) * 512]])
                P.op("act", lambda e, pp=pp, s=s: e.copy(out=vtok[:, s, :], in_=pp), reads=[pp], writes=[vtok[:, s, :]])
                if deferred:
                    deferred.pop(0)()
            release_blocks()
            for blk in range(2):
                s_g = next_block("gg%d" % blk)
                for jj in range(2):
                    pp = bank2()
                    mm_group(pp[:, 0:512], s_g, 2 * jj, hTt)
                    mm_group(pp[:, 512:1024], s_g, 2 * jj + 1, hTt)
                    dst = sgg[:, 4 * blk + 2 * jj:4 * blk + 2 * jj + 2, :]
                    P.op("act", lambda e, pp=pp, dst=dst: e.activation(out=dst, in_=pp.rearrange("p (a t) -> p a t", t=512), func=AF.Silu), reads=[pp], writes=[dst])
                release_blocks()
            s_q = next_block("q")
            for hd in range(4):
                pb = bank()
                mm_group(pb, s_q, hd, hTt)
                P.op("dve", lambda e, pb=pb, hd=hd: e.tensor_tensor(out=qd[:, hd, :], in0=pb, in1=geb[:, hd, :], op=ALU.mult),
                     reads=[pb, geb[:, hd, :]], writes=[qd[:, hd, :]])
            s_k = next_block("k")
            for hd in range(4):
                pb = bank()
                mm_group(pb, s_k, hd, hTt)
                P.op("dve", lambda e, pb=pb, hd=hd: e.tensor_tensor(out=kinvT[:, hd, :], in0=pb, in1=genb[:, hd, :], op=ALU.mult),
                     reads=[pb, genb[:, hd, :]], writes=[kinvT[:, hd, :]])
            release_blocks()
            if stop == "glain":
                finish([(0, qd[:, 1, :]), (128, kinvT[:, 2, :]), (256, vtok[:, 1, 0:512]), (384, sgg[:, 6, :])])

            def filler():
                pend = []
                pend_comb = []

                def pool_group_mm(g):
                    pg_ = pooled[g % 2]
                    sp_ = spg[g % 2]
                    for dc in range(2):
                        pb = bank()
                        for kc in range(2):
                            P.op("pe", lambda e, pb=pb, g=g, kc=kc, dc=dc: e.matmul(pb, lhsT=wpg[:, g, kc, dc * 128:(dc + 1) * 128], rhs=pg_[:, kc, :],
                                                                                   start=(kc == 0), stop=(kc == 1)),
                                 reads=[wpg[:, g, kc, dc * 128:(dc + 1) * 128], pg_[:, kc, :]], writes=[pb])
                        P.op("dve", lambda e, pb=pb, g=g, dc=dc: e.tensor_tensor(out=ypool[:, 2 * g + dc, :], in0=pb, in1=sp_[:, dc, :], op=ALU.mult),
                             reads=[pb, sp_[:, dc, :]], writes=[ypool[:, 2 * g + dc, :]])

                for blk in range(2):
                    s_pv = next_block("pv%d" % blk)
                    s_pg = next_block("pg%d" % blk)
                    for gg_ in range(2):
                        g = 2 * blk + gg_
                        w = 2 ** (g + 1)
                        pv_ = pvh[g % 2]
                        while pend_comb:
                            pend_comb.pop(0)()
                        P.op("pool", lambda e, pv_=pv_, g=g: e.tensor_copy(out=pv_[:, :, 0:16], in_=hist[:, 2 * g:2 * g + 2, :]),
                             reads=[hist[:, 2 * g:2 * g + 2, :]], writes=[pv_[:, :, 0:16]])
                        for cc in range(2):
                            j = 2 * gg_ + cc
                            pb = bank()
                            mm_group(pb, s_pv, j, hTt)
                            P.op("act", lambda e, pb=pb, pv_=pv_, cc=cc: e.copy(out=pv_[:, cc, 16:528], in_=pb), reads=[pb], writes=[pv_[:, cc, 16:528]])
                            yield
                        P.op("pool", lambda e, pv_=pv_, g=g: e.tensor_copy(out=hist[:, 2 * g:2 * g + 2, :], in_=pv_[:, :, 512:528]),
                             reads=[pv_[:, :, 512:528]], writes=[hist[:, 2 * g:2 * g + 2, :]])
                        src = pv_
                        lo = 16 - (w - 1)
                        m = 1
                        bufs = [tA, tB]
                        bi = 0
                        while m < w:
                            lo2 = lo + m
                            dst = bufs[bi]
                            eng = "dve"
                            P.op(eng, lambda e, dst=dst, src=src, lo2=lo2, m=m: e.tensor_tensor(out=dst[:, :, lo2:528], in0=src[:, :, lo2:528],
                                                                                               in1=src[:, :, lo2 - m:528 - m], op=ALU.add),
                                 reads=[src[:, :, lo2 - m:528]], writes=[dst[:, :, lo2:528]])
                            src = dst
                            lo = lo2
                            m *= 2
                            bi ^= 1
                        pg_ = pooled[g % 2]

                        def combine(src=src, pv_=pv_, pg_=pg_, w=w, g=g):
                            P.op("dve", lambda e: e.scalar_tensor_tensor(out=pg_, in0=src[:, :, 16:528], scalar=1.0 / w, in1=pv_[:, :, 16:528],
                                                                         op0=ALU.mult, op1=ALU.subtract),
                                 reads=[src[:, :, 16:528], pv_[:, :, 16:528]], writes=[pg_])
                            if ti == 0:
                                nfix = w - 1
                                tmpf = stat[:, 160:192].rearrange("p (c t) -> p c t", t=16)
                                P.op("dve", lambda e: e.tensor_tensor(out=tmpf[:, :, 0:nfix], in0=src[:, :, 16:16 + nfix],
                                                                      in1=invc[:, g:g + 1, 0:nfix].to_broadcast([128, 2, nfix]), op=ALU.mult),
                                     reads=[src[:, :, 16:16 + nfix], invc], writes=[tmpf])
                                P.op("dve", lambda e: e.tensor_tensor(out=pg_[:, :, 0:nfix], in0=tmpf[:, :, 0:nfix], in1=pv_[:, :, 16:16 + nfix], op=ALU.subtract),
                                     reads=[tmpf, pv_[:, :, 16:16 + nfix]], writes=[pg_[:, :, 0:nfix]])

                        pend_comb.append(combine)
                        sp_ = spg[g % 2]
                        for cc in range(2):
                            j = 2 * gg_ + cc
                            pb = bank()
                            mm_group(pb, s_pg, j, hTt)
                            P.op("act", lambda e, pb=pb, sp_=sp_, cc=cc: e.activation(out=sp_[:, cc, :], in_=pb, func=AF.Silu), reads=[pb], writes=[sp_[:, cc, :]])
                            yield
                        if pend:
                            pool_group_mm(pend.pop())
                            yield
                        pend.append(g)
                    release_blocks()
                while pend_comb:
                    pend_comb.pop(0)()
                if stop == "pool":
                    if pend:
                        pool_group_mm(pend.pop())
                    finish([(0, ypool[:, 0, :]), (128, ypool[:, 3, :]), (256, ypool[:, 7, :])])
                for blk in range(2):
                    s_m = next_block("mgp%d" % blk)
                    for j in range(4):
                        pb = bank()
                        mm_group(pb, s_m, j, hTt)
                        P.op("act", lambda e, pb=pb, blk=blk, j=j: e.activation(out=sgp[blk][:, j, :], in_=pb, func=AF.Sigmoid), reads=[pb], writes=[sgp[blk][:, j, :]])
                        yield
                    if pend:
                        pool_group_mm(pend.pop())
                        yield
                    s_w = next_block("wpo%d" % blk)
                    for j in range(4):
                        pb = bank()
                        mm_group(pb, s_w, j, ypool)
                        P.op("dve", lambda e, pb=pb, blk=blk, j=j: e.tensor_tensor(out=t1[:, 4 * blk + j, :], in0=pb, in1=sgp[blk][:, j, :], op=ALU.mult),
                             reads=[pb, sgp[blk][:, j, :]], writes=[t1[:, 4 * blk + j, :]])
                        yield
                    release_blocks()
                if stop == "pphase":
                    finish([(0, t1[:, 0, :]), (128, t1[:, 5, :])])

            fill = filler()

            def adv(n):
                for _ in range(n):
                    try:
                        next(fill)
                    except StopIteration:
                        return

            def S1(s):
                cs = slice(s * 128, (s + 1) * 128)
                kt = ktok[s % 2]
                sc_ = scs[s % 2]
                pT = bank().bitcast(BF16)[:, 0:512].rearrange("p (h t) -> p h t", t=128)
                for hd in range(4):
                    P.op("pe", lambda e, pT=pT, hd=hd, cs=cs: e.transpose(out=pT[:, hd, :], in_=kinvT[:, hd, cs], identity=ident),
                         reads=[kinvT[:, hd, cs], ident], writes=[pT[:, hd, :]])
                P.op("act", lambda e, pT=pT, kt=kt: e.copy(out=kt, in_=pT), reads=[pT], writes=[kt])
                pS = bank().rearrange("p (h t) -> p h t", t=128)
                for hd in range(4):
                    P.op("pe", lambda e, pS=pS, hd=hd, cs=cs: e.matmul(pS[:, hd, :], lhsT=kinvT[:, hd, cs], rhs=qd[:, hd, cs], start=True, stop=True),
                         reads=[kinvT[:, hd, cs], qd[:, hd, cs]], writes=[pS[:, hd, :]])
                P.op("dve", lambda e, pS=pS, sc_=sc_: e.tensor_tensor(out=sc_, in0=pS, in1=trimask.unsqueeze(1).to_broadcast([128, 4, 128]), op=ALU.mult),
                     reads=[pS, trimask], writes=[sc_])

            def S2(s):
                cs = slice(s * 128, (s + 1) * 128)
                kt = ktok[s % 2]
                sc_ = scs[s % 2]
                on_ = on[s % 2]
                pO = bank2()
                for hd in range(4):
                    oo = pO[:, hd * 256:(hd + 1) * 256]
                    P.op("pe", lambda e, oo=oo, hd=hd, s=s, sc_=sc_: e.matmul(oo, lhsT=sc_[:, hd, :], rhs=vtok[:, s, hd * 256:(hd + 1) * 256], start=True, stop=False),
                         reads=[sc_[:, hd, :], vtok[:, s, hd * 256:(hd + 1) * 256]], writes=[oo])
                    P.op("pe", lambda e, oo=oo, hd=hd, cs=cs: e.matmul(oo, lhsT=qd[:, hd, cs], rhs=S_bf[:, hd, :], start=False, stop=True),
                         reads=[qd[:, hd, cs], S_bf[:, hd, :]], writes=[oo])
                pD = bank2()
                for hd in range(4):
                    dd = pD[:, hd * 256:(hd + 1) * 256]
                    P.op("pe", lambda e, dd=dd, hd=hd, s=s, kt=kt: e.matmul(dd, lhsT=kt[:, hd, :], rhs=vtok[:, s, hd * 256:(hd + 1) * 256], start=True, stop=True),
                         reads=[kt[:, hd, :], vtok[:, s, hd * 256:(hd + 1) * 256]], writes=[dd])
                osb = yo[s % 4]
                P.op("act", lambda e, pO=pO, osb=osb: e.copy(out=osb, in_=pO), reads=[pO], writes=[osb])
                A2 = A_f.rearrange("p h v -> p (h v)")
                P.op("dve", lambda e, pD=pD, A2=A2: e.tensor_tensor(out=A2, in0=pD, in1=A2, op=ALU.add), reads=[pD, A_f], writes=[A_f])
                for hd in range(4):
                    P.op("act", lambda e, hd=hd, s=s: e.activation(out=S_bf[:, hd, :], in_=A_f[:, hd, :], func=AF.Copy, scale=decay[:, hd, s:s + 1]),
                         reads=[A_f[:, hd, :], decay], writes=[S_bf[:, hd, :]])
                P.op("dve", lambda e, s=s: e.tensor_tensor(out=A_f, in0=A_f, in1=decay[:, :, s:s + 1].to_broadcast([128, 4, 256]), op=ALU.mult),
                     reads=[A_f, decay], writes=[A_f])
                ssq = stat[:, 64 + 16 * (s % 2):68 + 16 * (s % 2)]
                rsq = stat[:, 68 + 16 * (s % 2):72 + 16 * (s % 2)]
                for hd in range(4):
                    P.op("act", lambda e, osb=osb, hd=hd: e.activation(out=junk[:, hd * 256:(hd + 1) * 256], in_=osb[:, hd * 256:(hd + 1) * 256], func=AF.Square,
                                                                      accum_out=ssq[:, hd:hd + 1]),
                         reads=[osb[:, hd * 256:(hd + 1) * 256]], writes=[junk[:, hd * 256:(hd + 1) * 256], ssq[:, hd:hd + 1]])
                P.op("pool", lambda e: e.tensor_scalar(out=rsq, in0=ssq, scalar1=1.0 / 256, scalar2=EPS, op0=ALU.mult, op1=ALU.add), reads=[ssq], writes=[rsq])
                P.op("pool", lambda e: e.tensor_tensor(out=rsq, in0=rsq, in1=mhalf[:, 0:4], op=ALU.pow), reads=[rsq, mhalf], writes=[rsq])

            def S2b(s):
                on_ = on[s % 2]
                osb = yo[s % 4]
                rsq = stat[:, 68 + 16 * (s % 2):72 + 16 * (s % 2)]
                for hd in range(4):
                    hs = slice(hd * 256, (hd + 1) * 256)
                    P.op("dve", lambda e, osb=osb, on_=on_, hd=hd, hs=hs: e.scalar_tensor_tensor(out=on_[:, hs], in0=osb[:, hs], scalar=rsq[:, hd:hd + 1], in1=gh_bc,
                                                                                                op0=ALU.mult, op1=ALU.mult),
                         reads=[osb[:, hs], rsq, gh_bc], writes=[on_[:, hs]])

            def S3(s):
                cs = slice(s * 128, (s + 1) * 128)
                on_ = on[s % 2]
                pT2 = bank().bitcast(BF16).rearrange("p (k t) -> p k t", t=128)
                for kc in range(8):
                    P.op("pe", lambda e, pT2=pT2, kc=kc, on_=on_: e.transpose(out=pT2[:, kc, :], in_=on_[:, kc * 128:(kc + 1) * 128], identity=ident),
                         reads=[on_[:, kc * 128:(kc + 1) * 128], ident], writes=[pT2[:, kc, :]])
                P.op("dve", lambda e, pT2=pT2, cs=cs: e.tensor_tensor(out=yglaT[:, :, cs], in0=pT2, in1=sgg[:, :, cs], op=ALU.mult),
                     reads=[pT2, sgg[:, :, cs]], writes=[yglaT[:, :, cs]])

            nxt = ti + 1 < NT
            S1(0)
            if nxt:
                XA(ti + 1, 0)
            adv(2)
            for s in range(NSUB):
                S2(s)
                if s + 1 < NSUB:
                    S1(s + 1)
                adv(2)
                S2b(s)
                adv(2)
                if s >= 1:
                    S3(s - 1)
                adv(2)
                if nxt:
                    XB(ti + 1, s)
                    if s + 1 < NSUB:
                        XA(ti + 1, s + 1)
                adv(2)
            adv(2)
            S3(NSUB - 1)
            adv(10 ** 6)
            if stop == "gla":
                finish([(0, yglaT[:, 0, :]), (128, yglaT[:, 3, :]), (256, yglaT[:, 6, :]), (384, A_f[:, 2, :])])
            for blk in range(2):
                s_m = next_block("mgg%d" % blk)
                for jj in range(2):
                    pp = bank2()
                    mm_group(pp[:, 0:512], s_m, 2 * jj, hTt)
                    mm_group(pp[:, 512:1024], s_m, 2 * jj + 1, hTt)
                    dst = sgg2[blk][:, 2 * jj:2 * jj + 2, :]
                    P.op("act", lambda e, pp=pp, dst=dst: e.activation(out=dst, in_=pp.rearrange("p (a t) -> p a t", t=512), func=AF.Sigmoid), reads=[pp], writes=[dst])
                s_w = next_block("wgo%d" % blk)
                for j in range(4):
                    pb = bank()
                    mm_group(pb, s_w, j, yglaT)
                    tg = tmpG[j % 2]
                    P.op("dve", lambda e, pb=pb, blk=blk, j=j, tg=tg: e.tensor_tensor(out=tg, in0=pb, in1=sgg2[blk][:, j, :], op=ALU.mult),
                         reads=[pb, sgg2[blk][:, j, :]], writes=[tg])
                    P.op("dve", lambda e, blk=blk, j=j, tg=tg: e.tensor_tensor(out=merged[:, 4 * blk + j, :], in0=tg, in1=t1[:, 4 * blk + j, :], op=ALU.add),
                         reads=[tg, t1[:, 4 * blk + j, :]], writes=[merged[:, 4 * blk + j, :]])
                release_blocks()
            s_o = [next_block("wo0"), next_block("wo1")]
            if ti == NT - 1:
                hfree = hT[(ti + 1) % 2]
                xlast = [xt[0], xt[1], A.view(hfree.offset * 2, [128, D], F32), A.view(hfree.offset * 2 + 4096, [128, D], F32)]
                for s in range(NSUB):
                    r0 = ti * T + s * 128
                    P.dma("sp", "xl%d" % s, lambda e, xl_=xlast[s], r0=r0: [e.dma_start(out=xl_, in_=dr["x"][r0:r0 + 128, :])], writes=[xlast[s]])
            for s in range(NSUB):
                r0 = ti * T + s * 128
                yo_ = yo[s % 4]
                pp = bank2()
                for cb in range(2):
                    for k in range(8):
                        P.op("pe", lambda e, pp=pp, cb=cb, k=k, s=s: e.matmul(pp[:, cb * 512:(cb + 1) * 512], lhsT=merged[:, k, s * 128:(s + 1) * 128],
                                                                             rhs=s_o[cb][:, k, :], start=(k == 0), stop=(k == 7)),
                             reads=[merged[:, k, s * 128:(s + 1) * 128], s_o[cb][:, k, :]], writes=[pp[:, cb * 512:(cb + 1) * 512]])
                P.op("dve", lambda e, pp=pp, yo_=yo_: e.tensor_tensor(out=yo_, in0=pp, in1=gate_bc, op=ALU.mult), reads=[pp, gate_bc], writes=[yo_])
                if ti == NT - 1:
                    P.op("dve", lambda e, yo_=yo_, xl_=xlast[s]: e.tensor_tensor(out=yo_, in0=yo_, in1=xl_, op=ALU.add), reads=[yo_, xlast[s]], writes=[yo_])
                else:
                    P.dma("pool", "ya%d" % (s % 4), lambda e, yo_=yo_, r0=r0: [e.dma_start(out=yo_, in_=dr["x"][r0:r0 + 128, :], accum_op=ALU.add)],
                          reads=[yo_], writes=[yo_])

            def final_sub(s, ti=ti, stage=None):
                r0 = ti * T + s * 128
                yo_ = yo[s % 4]
                ssy = stat[:, 96 + 16 * s:97 + 16 * s]
                rsy = stat[:, 97 + 16 * s:98 + 16 * s]
                if stage in (None, 0):
                    P.op("act", lambda e, yo_=yo_, ssy=ssy: e.activation(out=junk, in_=yo_, func=AF.Square, accum_out=ssy), reads=[yo_], writes=[junk, ssy])
                    P.op("pool", lambda e, ssy=ssy, rsy=rsy: e.tensor_scalar(out=rsy, in0=ssy, scalar1=1.0 / D, scalar2=EPS, op0=ALU.mult, op1=ALU.add), reads=[ssy], writes=[rsy])
                    P.op("pool", lambda e, rsy=rsy: e.tensor_tensor(out=rsy, in0=rsy, in1=mhalf[:, 0:1], op=ALU.pow), reads=[rsy, mhalf], writes=[rsy])
                if stage in (None, 1):
                    P.op("dve", lambda e, yo_=yo_, rsy=rsy: e.scalar_tensor_tensor(out=yo_, in0=yo_, scalar=rsy, in1=gfin_bc, op0=ALU.mult, op1=ALU.mult),
                         reads=[yo_, rsy, gfin_bc], writes=[yo_])
                if stage in (None, 2):
                    final_evs[s % 4] = P.dma("pool", "st%d" % (s % 4), lambda e, yo_=yo_, r0=r0: [e.dma_start(out=y_out[r0:r0 + 128, :], in_=yo_)], reads=[yo_])

            last_final[0] = final_sub
            for s in range(NSUB):
                deferred.append(lambda s=s: final_sub(s))
            release_blocks()

        final_evs = {}
        deferred = []
        last_final = [None]
        try:
            if stop == "setup":
                finish([(0, g1_bc), (128, gate_bc), (256, gfin_bc), (384, shift_col)])
            XA2(0, 0)
            for s_ in range(NSUB):
                if s_ + 1 < NSUB:
                    XA2(0, s_ + 1)
                XB(0, s_)
            if stop == "x0":
                finish([])
            for ti in range(NT):
                tile_prog(ti)
            deferred.clear()
            for st_ in range(3):
                for s_ in range(NSUB):
                    last_final[0](s_, stage=st_)
            for ev in final_evs.values():
                P.wait("pool", ev)
            P.emit()
        except _Stop:
            pass
    return nc


_NC_CACHE = {}


def _get_nc(NT):
    if NT not in _NC_CACHE:
        _NC_CACHE[NT] = build_nc(NT)
    return _NC_CACHE[NT]


def make_in_maps(inputs, NT, cores):
    f = lambda a: np.ascontiguousarray(np.asarray(a, dtype=np.float32))
    shared = dict(
        g_norm=f(inputs["g_norm"][0]), w_ada=f(inputs["w_ada"][0]), b_ada=f(inputs["b_ada"][0]), w_in=f(inputs["w_in"][0]),
        w_pool_group=f(inputs["w_pool_group"][0]), pool_scale=f(inputs["pool_scale"][0]), w_alpha_up=f(inputs["w_alpha_up"][0]),
        b_alpha=f(inputs["b_alpha"][0]), g_gla_head=f(inputs["g_gla_head"][0]), w_pool_out=f(inputs["w_pool_out"][0]),
        w_gla_out=f(inputs["w_gla_out"][0]), w_out=f(inputs["w_out"][0]), g_final=f(inputs["g_final"]))
    maps = []
    for b in cores:
        m = dict(shared)
        m["x"] = f(inputs["x"][b, :NT * T])
        m["c"] = f(inputs["c"][b])
        maps.append(m)
    return maps


def kernel(**inputs):
    NT = 4096 // T
    nc = _get_nc(NT)
    in_maps = make_in_maps(inputs, NT, list(range(8)))
    res = run_bass_kernel_spmd(nc, in_maps, core_ids=list(range(8)))
    out = np.stack([np.asarray(res.results[b]["y"], dtype=np.float32) for b in range(8)], axis=0)
    return out
```
